# Optimizing a Trainium2 kernel written in Bass

```python
import jax, jax.numpy as jnp
from jax import lax
import numpy as np

D_MODEL = 1024
BATCH = 4
SEQ = 8192
DEPTH = 2

CHUNK = 64
N_EVEN = (DEPTH + 1) // 2
N_ODD = DEPTH // 2

A_WIDTH = D_MODEL // 2
A_HEADS = 4
A_HEAD_DIM = A_WIDTH // A_HEADS
A_BLOCK = 128
B_WIDTH = D_MODEL // 2
POOL_WINDOWS = (2, 4, 8, 16)
B_GROUPS = len(POOL_WINDOWS)
B_GROUP_DIM = B_WIDTH // B_GROUPS
EVEN_IN = 2 * A_WIDTH + B_WIDTH
EVEN_MIX = A_WIDTH + B_WIDTH

C_WIDTH = D_MODEL // 2
C_KERNEL = 31
D_WIDTH = D_MODEL // 2
D_KERNEL = 3
ODD_IN = 2 * C_WIDTH + 3 * D_WIDTH
ODD_MIX = C_WIDTH + D_WIDTH

D_FF = -(-8 * D_MODEL // (3 * 256)) * 256
EPS = 1e-6

kernel_name = "hybrid_gmlp_pool_conformer_shortconv_trunk"


def rmsnorm(x, g):
    xf = x.astype(jnp.float32)
    y = xf * lax.rsqrt(jnp.mean(xf * xf, axis=-1, keepdims=True) + EPS)
    return (y * g.astype(jnp.float32)).astype(x.dtype)


def layernorm(x, g, b):
    xf = x.astype(jnp.float32)
    mu = jnp.mean(xf, axis=-1, keepdims=True)
    var = jnp.mean(jnp.square(xf - mu), axis=-1, keepdims=True)
    y = (xf - mu) * lax.rsqrt(var + EPS)
    return (y * g.astype(jnp.float32) + b.astype(jnp.float32)).astype(x.dtype)


def causal_depthwise_conv(x, w):
    k, c = w.shape
    return lax.conv_general_dilated(
        x, w[:, None, :].astype(x.dtype), window_strides=(1,), padding=[(k - 1, 0)],
        dimension_numbers=("NWC", "WIO", "NWC"), feature_group_count=c)


def mixer_a(u, v, w_s, b_s, ln_g, ln_b):
    bsz, s, _ = u.shape
    v = layernorm(v, ln_g, ln_b).reshape(bsz, s // A_BLOCK, A_BLOCK, A_HEADS, A_HEAD_DIM)
    pos = jnp.arange(A_BLOCK)
    mask = (pos[:, None] // CHUNK) >= (pos[None, :] // CHUNK)
    w = jnp.where(mask[None], w_s, jnp.zeros_like(w_s))
    sv = jnp.einsum("hij,bnjhd->bnihd", w, v) + b_s.T[None, None, :, :, None]
    return u * sv.reshape(bsz, s, A_WIDTH)


def mixer_b(x, w_pool, scale):
    bsz, s, _ = x.shape
    xf = x.astype(jnp.float32)
    csum = jnp.cumsum(xf, axis=1)
    t = jnp.arange(s)
    outs = []
    for gi, win in enumerate(POOL_WINDOWS):
        sl = slice(gi * B_GROUP_DIM, (gi + 1) * B_GROUP_DIM)
        cg = csum[..., sl]
        shifted = jnp.pad(cg, ((0, 0), (win, 0), (0, 0)))[:, :s]
        cnt = jnp.minimum(t + 1, win).astype(jnp.float32)[None, :, None]
        outs.append((cg - shifted) / cnt - xf[..., sl])
    pooled = jnp.stack(outs, axis=2).astype(x.dtype)
    y = jnp.einsum("bsgc,gcd->bsgd", pooled, w_pool).reshape(bsz, s, B_WIDTH)
    return y * scale


def mixer_c(a, g, w_dw, b_dw, ln_g, ln_b):
    h = a * jax.nn.sigmoid(g)
    h = causal_depthwise_conv(h, w_dw) + b_dw
    h = layernorm(h, ln_g, ln_b)
    return jax.nn.silu(h)


def mixer_d(bg, cg, xin, w_dw):
    return bg * causal_depthwise_conv(cg * xin, w_dw)


def swiglu(h, wg, wu, wd):
    return (jax.nn.silu(h @ wg) * (h @ wu)) @ wd


def setup_inputs(seed: int = 0) -> dict:
    key = jax.random.key(seed)
    ks = jax.random.split(key, 24)
    f = jnp.float32

    def nrm(k, shape, fan_in):
        return jax.random.normal(k, shape, f) * (fan_in ** -0.5)

    def gain(k, shape):
        return jnp.ones(shape, f) + 0.02 * jax.random.normal(k, shape, f)

    def small(k, shape):
        return 0.02 * jax.random.normal(k, shape, f)

    return {
        "x": jax.random.normal(ks[0], (BATCH, SEQ, D_MODEL), f),
        "even_w_in": nrm(ks[1], (N_EVEN, D_MODEL, EVEN_IN), D_MODEL),
        "even_w_out": nrm(ks[2], (N_EVEN, EVEN_MIX, D_MODEL), EVEN_MIX),
        "a_w_s": nrm(ks[3], (N_EVEN, A_HEADS, A_BLOCK, A_BLOCK), A_BLOCK),
        "a_b_s": gain(ks[4], (N_EVEN, A_HEADS, A_BLOCK)),
        "a_ln_g": gain(ks[5], (N_EVEN, A_WIDTH)),
        "a_ln_b": small(ks[6], (N_EVEN, A_WIDTH)),
        "b_w_pool": nrm(ks[7], (N_EVEN, B_GROUPS, B_GROUP_DIM, B_GROUP_DIM), B_GROUP_DIM),
        "b_scale": gain(ks[8], (N_EVEN, B_WIDTH)),
        "odd_w_in": nrm(ks[9], (N_ODD, D_MODEL, ODD_IN), D_MODEL),
        "odd_w_out": nrm(ks[10], (N_ODD, ODD_MIX, D_MODEL), ODD_MIX),
        "c_w_dw": nrm(ks[11], (N_ODD, C_KERNEL, C_WIDTH), C_KERNEL),
        "c_b_dw": small(ks[12], (N_ODD, C_WIDTH)),
        "c_ln_g": gain(ks[13], (N_ODD, C_WIDTH)),
        "c_ln_b": small(ks[14], (N_ODD, C_WIDTH)),
        "d_w_dw": nrm(ks[15], (N_ODD, D_KERNEL, D_WIDTH), D_KERNEL),
        "norm_mix_g": gain(ks[16], (DEPTH, D_MODEL)),
        "norm_ffn_g": gain(ks[17], (DEPTH, D_MODEL)),
        "ffn_w_gate": nrm(ks[18], (DEPTH, D_MODEL, D_FF), D_MODEL),
        "ffn_w_up": nrm(ks[19], (DEPTH, D_MODEL, D_FF), D_MODEL),
        "ffn_w_down": nrm(ks[20], (DEPTH, D_FF, D_MODEL), D_FF),
        "final_norm_g": gain(ks[21], (D_MODEL,)),
    }


def reference(x, even_w_in, even_w_out, a_w_s, a_b_s, a_ln_g, a_ln_b, b_w_pool, b_scale,
              odd_w_in, odd_w_out, c_w_dw, c_b_dw, c_ln_g, c_ln_b, d_w_dw,
              norm_mix_g, norm_ffn_g, ffn_w_gate, ffn_w_up, ffn_w_down, final_norm_g):
    h = x
    for layer in range(DEPTH):
        hn = rmsnorm(h, norm_mix_g[layer])
        if layer % 2 == 0:
            e = layer // 2
            z = hn @ even_w_in[e]
            za = jax.nn.gelu(z[..., :2 * A_WIDTH])
            u, v = za[..., :A_WIDTH], za[..., A_WIDTH:]
            zb = z[..., 2 * A_WIDTH:]
            ya = mixer_a(u, v, a_w_s[e], a_b_s[e], a_ln_g[e], a_ln_b[e])
            yb = mixer_b(zb, b_w_pool[e], b_scale[e])
            h = h + jnp.concatenate([ya, yb], axis=-1) @ even_w_out[e]
        else:
            o = layer // 2
            z = hn @ odd_w_in[o]
            ca = z[..., :C_WIDTH]
            cgt = z[..., C_WIDTH:2 * C_WIDTH]
            off = 2 * C_WIDTH
            dbg = z[..., off:off + D_WIDTH]
            dcg = z[..., off + D_WIDTH:off + 2 * D_WIDTH]
            dxin = z[..., off + 2 * D_WIDTH:]
            yc = mixer_c(ca, cgt, c_w_dw[o], c_b_dw[o], c_ln_g[o], c_ln_b[o])
            yd = mixer_d(dbg, dcg, dxin, d_w_dw[o])
            h = h + jnp.concatenate([yc, yd], axis=-1) @ odd_w_out[o]
        h = h + swiglu(rmsnorm(h, norm_ffn_g[layer]), ffn_w_gate[layer], ffn_w_up[layer],
                       ffn_w_down[layer])
    return rmsnorm(h, final_norm_g)
```

```python
import numpy as np
from contextlib import ExitStack
import concourse.bass as bass
import concourse.mybir as mybir
from concourse.bass_utils import run_bass_kernel_spmd

F32 = mybir.dt.float32
BF16 = mybir.dt.bfloat16
AF = mybir.ActivationFunctionType
ALU = mybir.AluOpType

NCORES = 8
BATCH = 4
SEQ = 8192
D = 1024
DFF = 2816
KC = D // 128
FC = DFF // 128
T = 512
EPS = 1e-6
POOL_WINDOWS = (2, 4, 8, 16)
CK = 31
DK = 3
SLOT = 4096
NSLOT = 4


class Op:
    __slots__ = ("eng", "fn", "is_dma", "dsem", "dcount", "deps", "signal", "sigidx", "idx", "rawwait")


class Prog:
    ENG = ("pe", "act", "dve", "pool", "sp")

    def __init__(self):
        self.ops = []
        self.last_w = {}
        self.readers = {}
        self.dma_tot = {}
        self.group_sems = set()

    def _add(self, eng, fn, reads, writes, is_dma=False, dsem=None):
        op = Op()
        op.eng, op.fn, op.is_dma, op.dsem = eng, fn, is_dma, dsem
        op.idx = len(self.ops)
        op.rawwait = None
        op.signal = False
        op.sigidx = None
        op.dcount = None
        if is_dma:
            self.dma_tot[dsem] = self.dma_tot.get(dsem, 0) + 16
            op.dcount = self.dma_tot[dsem]
        deps = {}
        rd = [k for k in reads]
        wr = [k for k in writes]
        for k in rd:
            if isinstance(k, tuple) and k[0] == "ps":
                wr.append(k)
        rd = [k for k in rd if not (isinstance(k, tuple) and k[0] == "ps")]
        for k in rd:
            w = self.last_w.get(k)
            if w is not None:
                deps[w.idx] = ("raw", w)
        for k in wr:
            w = self.last_w.get(k)
            if w is not None and w.idx not in deps:
                deps[w.idx] = ("raw" if (isinstance(k, tuple) and k[0] == "ps") else "waw", w)
            for r in self.readers.get(k, ()):
                if r.idx not in deps:
                    deps[r.idx] = ("war", r)
        for k in rd:
            self.readers.setdefault(k, []).append(op)
        for k in wr:
            self.last_w[k] = op
            self.readers[k] = []
        op.deps = list(deps.values())
        self.ops.append(op)
        return op

    def pe(self, fn, reads=(), writes=()):
        return self._add("pe", fn, reads, writes)

    def act(self, fn, reads=(), writes=()):
        return self._add("act", fn, reads, writes)

    def dve(self, fn, reads=(), writes=()):
        return self._add("dve", fn, reads, writes)

    def pool(self, fn, reads=(), writes=()):
        return self._add("pool", fn, reads, writes)

    def dma(self, eng, fn, dsem, reads=(), writes=()):
        return self._add(eng, fn, reads, writes, is_dma=True, dsem=dsem)

    def raw_wait(self, eng, dsem, value):
        op = self._add(eng, None, (), ())
        op.rawwait = (dsem, value)
        return op

    def finalize(self):
        need = []
        for op in self.ops:
            lst = []
            best = {}
            for kind, d in op.deps:
                if d.is_dma:
                    lst.append(d)
                    continue
                if d.eng == op.eng:
                    if op.eng == "pe":
                        continue
                    if kind != "raw":
                        continue
                if d.eng not in best or d.idx > best[d.eng].idx:
                    best[d.eng] = d
            for d in best.values():
                lst.append(d)
                d.signal = True
            need.append(lst)
        cnt = {e: 0 for e in self.ENG}
        for op in self.ops:
            if op.signal and not op.is_dma:
                cnt[op.eng] += 1
                op.sigidx = cnt[op.eng]
        self.need = need

    def emit(self, eng_name, eng, esem, dsems, final_waits=()):
        waited = {}
        for op, lst in zip(self.ops, self.need):
            if op.eng != eng_name:
                continue
            req = {}
            for d in lst:
                if d.is_dma:
                    key = ("d", d.dsem)
                    val = self.dma_tot[d.dsem] if d.dsem in self.group_sems else d.dcount
                else:
                    key = ("e", d.eng)
                    val = d.sigidx
                if val > req.get(key, 0):
                    req[key] = val
            for key, val in req.items():
                if waited.get(key, 0) >= val:
                    continue
                waited[key] = val
                sem = dsems[key[1]] if key[0] == "d" else esem[key[1]]
                eng.wait_ge(sem, val)
            if op.rawwait is not None:
                eng.wait_ge(dsems[op.rawwait[0]], op.rawwait[1])
                continue
            ins = op.fn(eng)
            if op.is_dma:
                ins.then_inc(dsems[op.dsem], 16)
            elif op.signal:
                ins.then_inc(esem[op.eng], 1)
        for key, val in final_waits:
            sem = dsems[key[1]] if key[0] == "d" else esem[key[1]]
            eng.wait_ge(sem, val)


def build_nc(tpc):
    assert tpc % T == 0
    NT = tpc // T
    nrows = 128 + tpc
    nc = bass.Bass("TRN2", target_bir_lowering=False)

    def din(name, shape, dt=F32):
        return nc.dram_tensor(name, list(shape), dt, kind="ExternalInput")

    x_d = din("x", [nrows, D])
    w_in0_d = din("w_in0", [D, 1536])
    w_out0_d = din("w_out0", [D, D])
    w_in1_d = din("w_in1", [D, 2560])
    w_out1_d = din("w_out1", [D, D])
    w_gate_d = din("w_gate", [2, D, DFF])
    w_up_d = din("w_up", [2, D, DFF])
    w_down_d = din("w_down", [2, DFF, D])
    gvec_d = din("gvec", [128, 5 * KC])
    wsT_d = din("wsT", [128, 512])
    maskT_d = din("maskT", [128, 128])
    alngc_d = din("alngc", [128, 4])
    alnbc_d = din("alnbc", [128, 4])
    bsbc_d = din("bsbc", [128, 512])
    wpool_d = din("wpool", [128, 512])
    bscale_d = din("bscale", [128, 4])
    pgen_d = din("pgen", [128, 512])
    pfirst_d = din("pfirst", [128, 512])
    pprev_d = din("pprev", [128, 512])
    cw_d = din("cw", [128, 4 * CK])
    dw_d = din("dw", [128, 4 * DK])
    cvec_d = din("cvec", [128, 12])
    ident_d = din("ident", [128, 128])
    out_d = nc.dram_tensor("out", [tpc, D], F32, kind="ExternalOutput")

    s_in0, s_out0, s_in1, s_out1 = w_in0_d, w_out0_d, w_in1_d, w_out1_d
    s_gate, s_up, s_down = w_gate_d, w_up_d, w_down_d

    P = Prog()
    es = ExitStack()

    def sb(name, shape, dt):
        return es.enter_context(nc.sbuf_tensor("sb_" + name, list(shape), dt))

    ident = sb("ident", [128, 128], F32)
    ones = sb("ones", [128, 128], BF16)
    gvec = sb("gvec", [128, 5 * KC], F32)
    wsT = sb("wsTb", [128, 4, 128], BF16)
    maskT = sb("maskT", [128, 128], F32)
    alngc = sb("alngc", [128, 4], F32)
    alnbc = sb("alnbc", [128, 4], F32)
    Bc = sb("Bc", [128, 4, 128], F32)
    ones_f = sb("ones_f", [128, 128], F32)
    wpool = sb("wpoolb", [128, 4, 128], BF16)
    bscale = sb("bscale", [128, 4], F32)
    pgen = sb("pgenb", [128, 4, 128], BF16)
    pfirst = sb("pfirstb", [128, 4, 128], BF16)
    pprev = sb("pprevb", [128, 4, 128], BF16)
    cw = sb("cw", [128, 4 * CK], F32)
    dw = sb("dw", [128, 4 * DK], F32)
    cvec = sb("cvec", [128, 12], F32)
    diagC = sb("diagC", [128, 4 * CK, 128], BF16)
    diagD = sb("diagD", [128, 4 * DK, 128], BF16)
    epsb = sb("epsb", [128, 1], F32)
    warm = sb("warm", [128, 2], F32)

    xin = [sb(f"xin{i}", [128, D], F32) for i in range(2)]
    ostage = xin
    hT2 = [sb("hTa", [128, KC, T], F32), sb("hTb", [128, KC, T], F32)]
    hsel = [0]

    def H():
        return hT2[hsel[0]]

    def kH(c):
        return ("hT", hsel[0], c)
    hn = sb("hn", [128, KC, T], BF16)
    mixT = sb("mixT", [128, KC, T], BF16)
    actT = sb("actT", [128, FC, T], BF16)
    hc = sb("hc", [128, 4, CK - 1 + T], BF16)
    pb = sb("pb", [128, 4, DK - 1 + T], BF16)
    zbt = [sb(f"zbt{i}", [128, 512], BF16) for i in range(3)]
    NF = 15
    Fall = sb("Fall", [128, NF, T], F32)
    Fp = [Fall[:, i, :] for i in range(NF)]
    uT4 = Fall[:, 0:4, :]
    sd_t = sb("sd_t", [128, T], F32)
    rstd_t = sb("rstd_t", [128, T], F32)
    stats = sb("stats", [128, 24], F32)
    mv = sb("mv", [128, 8], F32)
    mvs = sb("mvs", [128, 4], F32)
    mvr = sb("mvr", [128, 4], F32)
    mvn = sb("mvn", [128, 4], F32)
    wring = [sb(f"wr{i}", [128, SLOT], BF16) for i in range(NSLOT)]
    psb = [es.enter_context(nc.psum_tensor(f"ps{i}", [128, 512], F32)) for i in range(8)]

    def kF(i):
        return ("F", i)

    def kA(i):
        return ("actT", i)

    bank_ctr = [0]

    reserved = set()

    def next_bank():
        while True:
            b = bank_ctr[0] % 8
            bank_ctr[0] += 1
            if b not in reserved:
                return b

    def reserve_bank():
        b = next_bank()
        reserved.add(b)
        return b

    def release_bank(b):
        reserved.discard(b)

    esem = {e: es.enter_context(nc.semaphore(f"sem_{e}")) for e in ("pe", "act", "dve", "pool")}
    dsem_names = ["c", "xin0", "xin1", "ost0", "ost1"] + [f"w{i}" for i in range(NSLOT)] + \
                 ["cv0", "cv1", "cv2"]
    dsems = {n: es.enter_context(nc.semaphore(f"dsem_{n}")) for n in dsem_names}
    P.group_sems = {"c"}

    def cload(dst_ap, src_ap, key):
        P.dma("sp", lambda e, d=dst_ap, s=src_ap: e.dma_start(out=d, in_=s), "c", writes=[key])

    cload(ident[:], ident_d.ap(), "ident")
    cload(gvec[:], gvec_d.ap(), "gvec")
    cload(Fp[0][:], wsT_d.ap(), kF(0))
    cload(maskT[:], maskT_d.ap(), "maskT")
    cload(Fp[5], bsbc_d.ap(), kF(5))
    cload(alngc[:], alngc_d.ap(), "alngc")
    cload(alnbc[:], alnbc_d.ap(), "alnbc")
    cload(Fp[1][:], wpool_d.ap(), kF(1))
    cload(bscale[:], bscale_d.ap(), "bscale")
    cload(Fp[2][:], pgen_d.ap(), kF(2))
    cload(Fp[3][:], pfirst_d.ap(), kF(3))
    cload(Fp[4][:], pprev_d.ap(), kF(4))
    cload(cw[:], cw_d.ap(), "cw")
    cload(dw[:], dw_d.ap(), "dw")
    cload(cvec[:], cvec_d.ap(), "cvec")

    P.dve(lambda e: e.memset(ones[:], 1.0), writes=["ones"])
    P.dve(lambda e: e.memset(epsb[:], EPS), writes=["epsb"])
    P.dve(lambda e: e.memset(warm[:], 1.0), writes=["warm0"])
    for h in range(4):
        P.dve(lambda e, h=h: e.tensor_tensor(out=wsT[:, h, :], in0=Fp[0][:, h * 128:(h + 1) * 128],
                                             in1=maskT[:], op=ALU.mult),
              reads=[kF(0), "maskT"], writes=["wsT"])
    P.dve(lambda e: e.tensor_copy(out=wpool[:].rearrange("p g d -> p (g d)"), in_=Fp[1][:]),
          reads=[kF(1)], writes=["wpool"])
    P.dve(lambda e: e.tensor_copy(out=pgen[:].rearrange("p g d -> p (g d)"), in_=Fp[2][:]),
          reads=[kF(2)], writes=["pgen"])
    P.dve(lambda e: e.tensor_copy(out=pfirst[:].rearrange("p g d -> p (g d)"), in_=Fp[3][:]),
          reads=[kF(3)], writes=["pfirst"])
    P.dve(lambda e: e.tensor_copy(out=pprev[:].rearrange("p g d -> p (g d)"), in_=Fp[4][:]),
          reads=[kF(4)], writes=["pprev"])
    P.dve(lambda e: e.memset(ones_f[:], 1.0), writes=["ones_f"])
    for h in range(4):
        P.dve(lambda e, h=h: e.tensor_tensor(out=Fp[6][:, h * 128:(h + 1) * 128], in0=Fp[0][:, h * 128:(h + 1) * 128],
                                             in1=maskT[:], op=ALU.mult),
              reads=[kF(0), "maskT"], writes=[kF(6)])
    bkc = next_bank()
    P.pe(lambda e: e.matmul(psb[bkc][:], ones_f[:], Fp[6], start=True, stop=True),
         reads=["ones_f", kF(6)], writes=[("ps", bkc)])
    for h in range(4):
        P.dve(lambda e, h=h: e.scalar_tensor_tensor(out=Bc[:, h, :], in0=psb[bkc][:, h * 128:(h + 1) * 128],
                                                    scalar=alnbc[:, h:h + 1], in1=Fp[5][:, h * 128:(h + 1) * 128],
                                                    op0=ALU.mult, op1=ALU.add),
              reads=[("ps", bkc), "alnbc", kF(5)], writes=["Bc"])
    for i in range(4 * CK):
        P.dve(lambda e, i=i: e.tensor_scalar(out=diagC[:, i, :], in0=ident[:], scalar1=cw[:, i:i + 1],
                                             scalar2=None, op0=ALU.mult),
              reads=["ident", "cw"], writes=["diagC"])
    for i in range(4 * DK):
        P.dve(lambda e, i=i: e.tensor_scalar(out=diagD[:, i, :], in0=ident[:], scalar1=dw[:, i:i + 1],
                                             scalar2=None, op0=ALU.mult),
              reads=["ident", "dw"], writes=["diagD"])

    slot_ctr = [0]

    def wload(src_ap, ncols_elems, view_fn, src_key):
        s = slot_ctr[0] % NSLOT
        slot_ctr[0] += 1
        dst = view_fn(wring[s])
        P.dma("pool", lambda e, d=dst, a=src_ap: e.dma_start(out=d, in_=a), f"w{s}",
              reads=["ident", "cvec"], writes=[("w", s)])
        return s, dst

    def wunit_kn(src2d, col0, ncols, src_key):
        src = src2d[:, col0:col0 + ncols].rearrange("(k p) n -> p k n", p=128)
        return wload(src, 8 * ncols,
                     lambda w: w[:, 0:8 * ncols].rearrange("p (k n) -> p k n", k=8), src_key)

    def wunit_down(src2d, hf, q, src_key):
        src = src2d[hf * 11 * 128:(hf + 1) * 11 * 128, q * 256:(q + 1) * 256].rearrange("(i p) n -> p i n", p=128)
        return wload(src, 11 * 256,
                     lambda w: w[:, 0:11 * 256].rearrange("p (i n) -> p i n", i=11), src_key)

    def mm(out, lhsT, rhs, start, stop, reads, writes):
        P.pe(lambda e: e.matmul(out, lhsT, rhs, start=start, stop=stop), reads=reads, writes=writes)

    def load_x_block(blk_global, slot):
        r0 = blk_global * 128
        P.dma("sp", lambda e: e.dma_start(out=xin[slot][:], in_=x_d.ap()[r0:r0 + 128, :]), f"xin{slot}",
              writes=[("xin", slot)])

    def input_block(b, slot):
        cols = slice(b * 128, (b + 1) * 128)
        for half in range(2):
            bk = next_bank()
            for cc in range(4):
                c = half * 4 + cc
                P.pe(lambda e, c=c, cc=cc, bk=bk: e.transpose(psb[bk][:, cc * 128:(cc + 1) * 128],
                                                             xin[slot][:, c * 128:(c + 1) * 128], ident[:]),
                     reads=[("xin", slot), "ident"], writes=[("ps", bk)])
            src = psb[bk][:].rearrange("p (c t) -> p c t", c=4)
            dst = H()[:, half * 4:(half + 1) * 4, cols]
            keys = [kH(half * 4 + cc) for cc in range(4)]
            if half == 0:
                P.act(lambda e, s=src, d=dst: e.activation(out=d, in_=s, func=AF.Copy),
                      reads=[("ps", bk)], writes=keys)
            else:
                P.dve(lambda e, s=src, d=dst: e.tensor_copy(out=d, in_=s), reads=[("ps", bk)], writes=keys)

    def input_square(b):
        cols = slice(b * 128, (b + 1) * 128)
        P.act(lambda e, h=H(): e.activation(out=hn[:, 0:8, cols], in_=h[:, 0:8, cols], func=AF.Square),
              reads=[kH(c) for c in range(KC)], writes=[("hn", c) for c in range(KC)])

    def input_ssq(b, ssq_bank, first, last):
        cols = slice(b * 128, (b + 1) * 128)
        for c in range(KC):
            mm(psb[ssq_bank][:, cols], ones[:], hn[:, c, cols], first and c == 0, last and c == KC - 1,
               ["ones", ("hn", c)], [("ps", ssq_bank)])

    def recip(Tt):
        P.dve(lambda e: e.reciprocal(out=rstd_t[:, 0:Tt], in_=sd_t[:, 0:Tt]), reads=["sd"], writes=["rstd"])

    def warm_sqrt():
        P.act(lambda e: e.activation(out=warm[:, 1:2], in_=warm[:, 0:1], func=AF.Sqrt), reads=["warm0"], writes=["warm1"])

    def norm_finish(Tt, gidx, bk, out_bf16=True, out_tiles=None):
        P.act(lambda e: e.activation(out=sd_t[:, 0:Tt], in_=psb[bk][:, 0:Tt], func=AF.Sqrt, scale=1.0 / D,
                                     bias=epsb[:, 0:1]),
              reads=[("ps", bk), "epsb"], writes=["sd"])
        release_bank(bk)
        recip(Tt)
        for c in range(KC):
            g_ap = gvec[:, gidx * KC + c:gidx * KC + c + 1]
            if out_bf16:
                P.dve(lambda e, c=c, g=g_ap, h=H(): e.scalar_tensor_tensor(out=hn[:, c, 0:Tt], in0=h[:, c, 0:Tt], scalar=g,
                                                                           in1=rstd_t[:, 0:Tt], op0=ALU.mult, op1=ALU.mult),
                      reads=[kH(c), "rstd", "gvec"], writes=[("hn", c)])
            else:
                fi = out_tiles[c]
                P.dve(lambda e, c=c, g=g_ap, fi=fi, h=H(): e.scalar_tensor_tensor(out=Fp[fi][:, 0:Tt], in0=h[:, c, 0:Tt],
                                                                                  scalar=g, in1=rstd_t[:, 0:Tt],
                                                                                  op0=ALU.mult, op1=ALU.mult),
                      reads=[kH(c), "rstd", "gvec"], writes=[kF(fi)])

    class NormAcc:
        def __init__(self, Tt, sqbuf, sqkey):
            self.Tt, self.sqbuf, self.sqkey = Tt, sqbuf, sqkey
            self.bank = reserve_bank()
            self.pending = []
            self.n_mm = 0
            self.warmed = False

        def chunk_done(self, c):
            Tt = self.Tt
            P.act(lambda e, c=c, h=H(): e.activation(out=self.sqbuf[:, c, 0:Tt], in_=h[:, c, 0:Tt], func=AF.Square),
                  reads=[kH(c)], writes=[self.sqkey(c)])
            self.pending.append(c)
            if not self.warmed:
                warm_sqrt()
                self.warmed = True

        def flush(self, keep=0):
            while len(self.pending) > keep:
                c = self.pending.pop(0)
                mm(psb[self.bank][:, 0:self.Tt], ones[:], self.sqbuf[:, c, 0:self.Tt], self.n_mm == 0,
                   self.n_mm == KC - 1, ["ones", self.sqkey(c)], [("ps", self.bank)])
                self.n_mm += 1

    HN_KEYS = [("hn", c) for c in range(KC)]

    def proj_fm(Tt, wview, wslot, oc, bk, rhs_tile, rhs_keys, nk=KC):
        for k in range(nk):
            mm(psb[bk][:, 0:Tt], wview[:, k, oc * 128:(oc + 1) * 128], rhs_tile[:, k, 0:Tt], k == 0, k == nk - 1,
               [("w", wslot), rhs_keys[k]], [("ps", bk)])

    def proj_fm_kouter(Tt, wview, wslot, ocs, bks, rhs_tile, rhs_keys):
        for k in range(KC):
            for oc, bk in zip(ocs, bks):
                mm(psb[bk][:, 0:Tt], wview[:, k, oc * 128:(oc + 1) * 128], rhs_tile[:, k, 0:Tt], k == 0, k == KC - 1,
                   [("w", wslot), rhs_keys[k]], [("ps", bk)])

    def proj_tm(b, wview, wslot, bk, ncols=512):
        for k in range(KC):
            mm(psb[bk][:, 0:ncols], hn[:, k, b * 128:(b + 1) * 128], wview[:, k, 0:ncols], k == 0, k == KC - 1,
               [("w", wslot), ("hn", k)], [("ps", bk)])

    zb_ctr = [0]

    def layer0_mixer(Tt, first_real, is_halo):
        NBt = Tt // 128
        s_u, v_u = wunit_kn(s_in0.ap(), 0, 512, "s_in0")
        bks = [next_bank() for _ in range(4)]
        proj_fm_kouter(Tt, v_u, s_u, range(4), bks, hn, HN_KEYS)
        for oc in range(4):
            bk = bks[oc]
            P.act(lambda e, oc=oc, bk=bk: e.activation(out=uT4[:, oc, 0:Tt], in_=psb[bk][:, 0:Tt],
                                                       func=AF.Gelu_apprx_tanh),
                  reads=[("ps", bk)], writes=[kF(oc)])
        s_v, v_v = wunit_kn(s_in0.ap(), 512, 512, "s_in0")
        for b in range(NBt):
            bk = next_bank()
            proj_tm(b, v_v, s_v, bk)
            vt = 4 + b
            P.act(lambda e, bk=bk, vt=vt: e.activation(out=Fp[vt][:], in_=psb[bk][:], func=AF.Gelu_apprx_tanh),
                  reads=[("ps", bk)], writes=[kF(vt)])
            P.dve(lambda e, vt=vt, b=b: e.bn_stats(out=stats[:, 6 * b:6 * b + 6], in_=Fp[vt][:]),
                  reads=[kF(vt)], writes=[("stats", b)])
            P.dve(lambda e, b=b: e.bn_aggr(out=mv[:, 2 * b:2 * b + 2], in_=stats[:, 6 * b:6 * b + 6]),
                  reads=[("stats", b)], writes=[("mv", b)])
        mvv = mv[:, 0:2 * NBt].rearrange("p (b t) -> p b t", t=2)[:, :, 1:2]
        P.act(lambda e: e.activation(out=mvs[:, 0:NBt].rearrange("p (b o) -> p b o", o=1), in_=mvv, func=AF.Sqrt,
                                     bias=epsb[:, 0:1]),
              reads=[("mv", b) for b in range(NBt)] + ["epsb"], writes=["mvs"])
        P.dve(lambda e: e.reciprocal(out=mvr[:, 0:NBt], in_=mvs[:, 0:NBt]), reads=["mvs"], writes=["mvr"])
        mvm = mv[:, 0:2 * NBt].rearrange("p (b t) -> p b t", t=2)[:, :, 0:1]
        P.dve(lambda e: e.scalar_tensor_tensor(out=mvn[:, 0:NBt].rearrange("p (b o) -> p b o", o=1), in0=mvm, scalar=-1.0,
                                               in1=mvr[:, 0:NBt].rearrange("p (b o) -> p b o", o=1),
                                               op0=ALU.mult, op1=ALU.mult),
              reads=[("mv", b) for b in range(NBt)] + ["mvr"], writes=["mvn"])
        s_z, v_z = wunit_kn(s_in0.ap(), 1024, 512, "s_in0")

        def pooling(b, cur, prev):
            bk2 = next_bank()
            pc = pfirst if (first_real and b == 0) else pgen
            pck = "pfirst" if (first_real and b == 0) else "pgen"
            for g in range(4):
                mm(psb[bk2][:, g * 128:(g + 1) * 128], zbt[cur][:, g * 128:(g + 1) * 128], pc[:, g, :],
                   g == 0, False if not is_halo else (g == 3), [("zbt", cur), pck], [("ps", bk2)])
            if not is_halo:
                for g in range(4):
                    mm(psb[bk2][:, g * 128:(g + 1) * 128], zbt[prev][:, g * 128:(g + 1) * 128], pprev[:, g, :],
                       False, g == 3, [("zbt", prev), "pprev"], [("ps", bk2)])
            P.act(lambda e, bk2=bk2, b=b: e.activation(out=actT[:, 8:12, b * 128:(b + 1) * 128],
                                                       in_=psb[bk2][:].rearrange("p (g t) -> p g t", g=4),
                                                       func=AF.Copy),
                  reads=[("ps", bk2)], writes=[kA(8), kA(9), kA(10), kA(11)])

        pend = None
        for b in range(NBt):
            bk = next_bank()
            proj_tm(b, v_z, s_z, bk)
            cur = zb_ctr[0] % 3
            prev = (zb_ctr[0] - 1) % 3
            zb_ctr[0] += 1
            P.act(lambda e, bk=bk, cur=cur: e.activation(out=zbt[cur][:], in_=psb[bk][:], func=AF.Copy),
                  reads=[("ps", bk)], writes=[("zbt", cur)])
            if pend is not None:
                pooling(*pend)
            pend = (b, cur, prev)
        pooling(*pend)
        for b in range(NBt):
            vt = 4 + b
            vn_i = 16 + (b % 2)
            P.act(lambda e, vt=vt, b=b, vn_i=vn_i: e.activation(out=actT[:, vn_i, :], in_=Fp[vt][:], func=AF.Identity,
                                                                scale=mvr[:, b:b + 1], bias=mvn[:, b:b + 1]),
                  reads=[kF(vt), "mvr", "mvn"], writes=[kA(vn_i)])
            bk2 = next_bank()
            for h in range(4):
                mm(psb[bk2][:, h * 128:(h + 1) * 128], actT[:, vn_i, h * 128:(h + 1) * 128], wsT[:, h, :],
                   h == 0, h == 3, [kA(vn_i), "wsT"], [("ps", bk2)])
            svt = 8 + (b % 2)
            for h in range(4):
                P.dve(lambda e, h=h, bk2=bk2, svt=svt: e.scalar_tensor_tensor(
                    out=Fp[svt][:, h * 128:(h + 1) * 128], in0=psb[bk2][:, h * 128:(h + 1) * 128],
                    scalar=alngc[:, h:h + 1], in1=Bc[:, h, :], op0=ALU.mult, op1=ALU.add),
                    reads=[("ps", bk2), "alngc", "Bc"], writes=[kF(svt)])
            P.dve(lambda e, svt=svt, b=b: e.tensor_tensor(out=mixT[:, 0:4, b * 128:(b + 1) * 128],
                                                          in0=Fp[svt].rearrange("p (h t) -> p h t", h=4),
                                                          in1=uT4[:, :, b * 128:(b + 1) * 128], op=ALU.mult),
                  reads=[kF(svt), kF(0), kF(1), kF(2), kF(3)], writes=[("mixT", h) for h in range(4)])
        for g in range(4):
            bk = next_bank()
            mm(psb[bk][:, 0:Tt], wpool[:, g, :], actT[:, 8 + g, 0:Tt], True, True, ["wpool", kA(8 + g)], [("ps", bk)])
            P.act(lambda e, g=g, bk=bk: e.activation(out=mixT[:, 4 + g, 0:Tt], in_=psb[bk][:, 0:Tt], func=AF.Copy,
                                                     scale=bscale[:, g:g + 1]),
                  reads=[("ps", bk), "bscale"], writes=[("mixT", 4 + g)])

    MIX_KEYS = [("mixT", c) for c in range(KC)]

    def out_proj(Tt, s_out, key, gidx_next):
        na = NormAcc(Tt, actT, kA)
        for j in range(2):
            s_w, v_w = wunit_kn(s_out, j * 512, 512, key)
            for oc in range(4):
                c = j * 4 + oc
                bk = next_bank()
                proj_fm(Tt, v_w, s_w, oc, bk, mixT, MIX_KEYS)
                P.dve(lambda e, c=c, bk=bk, h=H(): e.tensor_tensor(out=h[:, c, 0:Tt], in0=psb[bk][:, 0:Tt],
                                                                   in1=h[:, c, 0:Tt], op=ALU.add),
                      reads=[("ps", bk), kH(c)], writes=[kH(c)])
                na.chunk_done(c)
                na.flush(keep=2)
        na.flush()
        norm_finish(Tt, gidx_next, na.bank)

    def ffn(Tt, layer, gidx_next, final=False, mid_hook=None):
        sg_ = s_gate.ap()[layer]
        su_ = s_up.ap()[layer]
        sd_ = s_down.ap()[layer]
        for j in range(6):
            ncols = 512 if j < 5 else 256
            s_g, v_g = wunit_kn(sg_, j * 512, ncols, f"s_g{layer}")
            s_u, v_u = wunit_kn(su_, j * 512, ncols, f"s_u{layer}")
            if j == 0:
                bgs = [next_bank() for _ in range(4)]
                proj_fm_kouter(Tt, v_g, s_g, range(4), bgs, hn, HN_KEYS)
            for oc in range(ncols // 128):
                fc = j * 4 + oc
                if j == 0:
                    bg = bgs[oc]
                else:
                    bg = next_bank()
                    proj_fm(Tt, v_g, s_g, oc, bg, hn, HN_KEYS)
                bu = next_bank()
                proj_fm(Tt, v_u, s_u, oc, bu, hn, HN_KEYS)
                st = 8 + (fc % 2)
                P.act(lambda e, bg=bg, st=st: e.activation(out=Fp[st][:, 0:Tt], in_=psb[bg][:, 0:Tt], func=AF.Silu),
                      reads=[("ps", bg)], writes=[kF(st)])
                P.dve(lambda e, bu=bu, st=st, fc=fc: e.tensor_tensor(out=actT[:, fc, 0:Tt], in0=psb[bu][:, 0:Tt],
                                                                     in1=Fp[st][:, 0:Tt], op=ALU.mult),
                      reads=[("ps", bu), kF(st)], writes=[kA(fc)])
        late = list(mid_hook()) if mid_hook is not None else []
        for _ in range(2):
            if len(late) > 1:
                late.pop(0)()
        na = NormAcc(Tt, mixT, lambda c: ("mixT", c))
        for q in range(4):
            b0 = next_bank()
            b1 = next_bank()
            bks = (b0, b1)
            for hf in range(2):
                s_d, v_d = wunit_down(sd_, hf, q, f"s_d{layer}")
                for i in range(11):
                    fc = hf * 11 + i
                    for o2 in range(2):
                        mm(psb[bks[o2]][:, 0:Tt], v_d[:, i, o2 * 128:(o2 + 1) * 128], actT[:, fc, 0:Tt],
                           fc == 0, fc == FC - 1, [("w", s_d), kA(fc)], [("ps", bks[o2])])
            na.flush()
            if late:
                late.pop(0)()
            for o2 in range(2):
                c = q * 2 + o2
                P.dve(lambda e, c=c, bk=bks[o2], h=H(): e.tensor_tensor(out=h[:, c, 0:Tt], in0=psb[bk][:, 0:Tt],
                                                                        in1=h[:, c, 0:Tt], op=ALU.add),
                      reads=[("ps", bks[o2]), kH(c)], writes=[kH(c)])
                na.chunk_done(c)
        while late:
            late.pop(0)()
        na.flush()
        if final:
            norm_finish(Tt, gidx_next, na.bank, out_bf16=False, out_tiles=list(range(8)))
        else:
            norm_finish(Tt, gidx_next, na.bank)

    def layer1_mixer(Tt, is_halo):
        H0 = CK - 1
        Q0 = DK - 1
        s_a, v_a = wunit_kn(s_in1.ap(), 0, 512, "s_in1")
        s_g, v_g = wunit_kn(s_in1.ap(), 512, 512, "s_in1")
        bas = [next_bank() for _ in range(4)]
        proj_fm_kouter(Tt, v_a, s_a, range(4), bas, hn, HN_KEYS)
        for c in range(4):
            ba = bas[c]
            bg = next_bank()
            proj_fm(Tt, v_g, s_g, c, bg, hn, HN_KEYS)
            st = c % 2
            P.act(lambda e, bg=bg, st=st: e.activation(out=Fp[st][:, 0:Tt], in_=psb[bg][:, 0:Tt], func=AF.Sigmoid),
                  reads=[("ps", bg)], writes=[kF(st)])
            P.dve(lambda e, ba=ba, st=st, c=c: e.tensor_tensor(out=hc[:, c, H0:H0 + Tt], in0=psb[ba][:, 0:Tt],
                                                               in1=Fp[st][:, 0:Tt], op=ALU.mult),
                  reads=[("ps", ba), kF(st)], writes=["hc"])
        if not is_halo:
            for c in range(4):
                bk = next_bank()
                for j in range(CK):
                    mm(psb[bk][:, 0:Tt], diagC[:, c * CK + j, :], hc[:, c, j:j + Tt], j == 0, j == CK - 1,
                       ["diagC", "hc"], [("ps", bk)])
                P.act(lambda e, c=c, bk=bk: e.activation(out=actT[:, 8 + c, 0:Tt], in_=psb[bk][:, 0:Tt],
                                                         func=AF.Identity, bias=cvec[:, c:c + 1]),
                      reads=[("ps", bk), "cvec"], writes=[kA(8 + c)])
                P.act(lambda e, c=c, bk=bk: e.activation(out=actT[:, 12 + c, 0:Tt], in_=psb[bk][:, 0:Tt],
                                                         func=AF.Square, bias=cvec[:, c:c + 1]),
                      reads=[("ps", bk), "cvec"], writes=[kA(12 + c)])
                P.act(lambda e, c=c, bk=bk: e.activation(out=Fp[6 + c][:, 0:Tt], in_=psb[bk][:, 0:Tt],
                                                         func=AF.Identity, bias=cvec[:, c:c + 1]),
                      reads=[("ps", bk), "cvec"], writes=[kF(6 + c)])
            b1 = next_bank()
            for c in range(4):
                mm(psb[b1][:, 0:Tt], ones[:], actT[:, 8 + c, 0:Tt], c == 0, c == 3, ["ones", kA(8 + c)], [("ps", b1)])
            b2 = next_bank()
            for c in range(4):
                mm(psb[b2][:, 0:Tt], ones[:], actT[:, 12 + c, 0:Tt], c == 0, c == 3, ["ones", kA(12 + c)], [("ps", b2)])
            P.dve(lambda e: e.tensor_scalar(out=Fp[10][:, 0:Tt], in0=psb[b1][:, 0:Tt], scalar1=1.0 / 512, scalar2=None,
                                            op0=ALU.mult), reads=[("ps", b1)], writes=[kF(10)])
            P.dve(lambda e: e.tensor_tensor(out=Fp[11][:, 0:Tt], in0=Fp[10][:, 0:Tt], in1=Fp[10][:, 0:Tt], op=ALU.mult),
                  reads=[kF(10)], writes=[kF(11)])
            P.dve(lambda e: e.scalar_tensor_tensor(out=Fp[12][:, 0:Tt], in0=psb[b2][:, 0:Tt], scalar=1.0 / 512,
                                                   in1=Fp[11][:, 0:Tt], op0=ALU.mult, op1=ALU.subtract),
                  reads=[("ps", b2), kF(11)], writes=[kF(12)])
            P.act(lambda e: e.activation(out=sd_t[:, 0:Tt], in_=Fp[12][:, 0:Tt], func=AF.Sqrt, bias=epsb[:, 0:1]),
                  reads=[kF(12), "epsb"], writes=["sd"])
            recip(Tt)
            for c in range(4):
                t1 = 13 + (c % 2)
                P.dve(lambda e, c=c, t1=t1: e.tensor_tensor(out=Fp[t1][:, 0:Tt], in0=Fp[6 + c][:, 0:Tt],
                                                            in1=Fp[10][:, 0:Tt], op=ALU.subtract),
                      reads=[kF(6 + c), kF(10)], writes=[kF(t1)])
                P.dve(lambda e, t1=t1: e.tensor_tensor(out=Fp[t1][:, 0:Tt], in0=Fp[t1][:, 0:Tt], in1=rstd_t[:, 0:Tt],
                                                       op=ALU.mult),
                      reads=[kF(t1), "rstd"], writes=[kF(t1)])
                P.act(lambda e, c=c, t1=t1: e.activation(out=mixT[:, c, 0:Tt], in_=Fp[t1][:, 0:Tt], func=AF.Silu,
                                                         scale=cvec[:, 4 + c:5 + c], bias=cvec[:, 8 + c:9 + c]),
                      reads=[kF(t1), "cvec"], writes=[("mixT", c)])
            s_b, v_b = wunit_kn(s_in1.ap(), 1024, 512, "s_in1")
            for c in range(4):
                bb = next_bank()
                proj_fm(Tt, v_b, s_b, c, bb, hn, HN_KEYS)
                P.act(lambda e, bb=bb, c=c: e.activation(out=Fp[2 + c][:, 0:Tt], in_=psb[bb][:, 0:Tt], func=AF.Copy),
                      reads=[("ps", bb)], writes=[kF(2 + c)])
        s_c, v_c = wunit_kn(s_in1.ap(), 1536, 512, "s_in1")
        s_x, v_x = wunit_kn(s_in1.ap(), 2048, 512, "s_in1")
        for c in range(4):
            bc = next_bank()
            proj_fm(Tt, v_c, s_c, c, bc, hn, HN_KEYS)
            bx = next_bank()
            proj_fm(Tt, v_x, s_x, c, bx, hn, HN_KEYS)
            st = c % 2
            P.act(lambda e, bc=bc, st=st: e.activation(out=Fp[st][:, 0:Tt], in_=psb[bc][:, 0:Tt], func=AF.Copy),
                  reads=[("ps", bc)], writes=[kF(st)])
            P.dve(lambda e, bx=bx, st=st, c=c: e.tensor_tensor(out=pb[:, c, Q0:Q0 + Tt], in0=psb[bx][:, 0:Tt],
                                                               in1=Fp[st][:, 0:Tt], op=ALU.mult),
                  reads=[("ps", bx), kF(st)], writes=["pb"])
        if not is_halo:
            for c in range(4):
                bk = next_bank()
                for j in range(DK):
                    mm(psb[bk][:, 0:Tt], diagD[:, c * DK + j, :], pb[:, c, j:j + Tt], j == 0, j == DK - 1,
                       ["diagD", "pb"], [("ps", bk)])
                P.dve(lambda e, c=c, bk=bk: e.tensor_tensor(out=mixT[:, 4 + c, 0:Tt], in0=psb[bk][:, 0:Tt],
                                                            in1=Fp[2 + c][:, 0:Tt], op=ALU.mult),
                      reads=[("ps", bk), kF(2 + c)], writes=[("mixT", 4 + c)])
        P.act(lambda e: e.activation(out=hc[:, :, 0:H0], in_=hc[:, :, Tt:Tt + H0], func=AF.Copy),
              reads=["hc"], writes=["hc"])
        P.act(lambda e: e.activation(out=pb[:, :, 0:Q0], in_=pb[:, :, Tt:Tt + Q0], func=AF.Copy),
              reads=["pb"], writes=["pb"])

    ost_ctr = [0]

    def final_out(Tt, row0):
        for b in range(Tt // 128):
            slot = ost_ctr[0] % 2
            ost_ctr[0] += 1
            for half in range(2):
                bk = next_bank()
                for cc in range(4):
                    c = half * 4 + cc
                    P.pe(lambda e, c=c, cc=cc, bk=bk, b=b: e.transpose(psb[bk][:, cc * 128:(cc + 1) * 128],
                                                                      Fp[c][:, b * 128:(b + 1) * 128], ident[:]),
                         reads=[kF(c), "ident"], writes=[("ps", bk)])
                dst = ostage[slot][:, half * 512:(half + 1) * 512]
                if half == 0:
                    P.act(lambda e, bk=bk, d=dst: e.activation(out=d, in_=psb[bk][:], func=AF.Copy),
                          reads=[("ps", bk)], writes=[("xin", slot)])
                else:
                    P.dve(lambda e, bk=bk, d=dst: e.tensor_copy(out=d, in_=psb[bk][:]),
                          reads=[("ps", bk)], writes=[("xin", slot)])
            r0 = row0 + b * 128
            P.dma("act", lambda e, slot=slot, r0=r0: e.dma_start(out=out_d.ap()[r0:r0 + 128, :], in_=ostage[slot][:]),
                  f"ost{slot}", reads=[("xin", slot)], writes=["out_dram"])

    xblk = [0]

    xissued = [0]

    def issue_x(upto):
        while xissued[0] < upto and xissued[0] < nrows // 128:
            load_x_block(xissued[0], xissued[0] % 2)
            xissued[0] += 1

    def load_tile_input(NBt, defer=False):
        bank = reserve_bank()
        hs = hsel[0]
        steps = []

        def mk_block(b):
            def step():
                save = hsel[0]
                hsel[0] = hs
                slot = xblk[0] % 2
                issue_x(xblk[0] + 1)
                input_block(b, slot)
                xblk[0] += 1
                if b > 0:
                    input_ssq(b - 1, bank, b - 1 == 0, False)
                input_square(b)
                if b == 0:
                    warm_sqrt()
                hsel[0] = save
            return step

        def finish():
            save = hsel[0]
            hsel[0] = hs
            input_ssq(NBt - 1, bank, NBt == 1, True)
            norm_finish(NBt * 128, 0, bank)
            hsel[0] = save

        for b in range(NBt):
            steps.append(mk_block(b))
        steps.append(finish)
        if defer:
            return steps
        for st in steps:
            st()
        return None

    def prefetch_next():
        hsel[0] ^= 1
        fin = load_tile_input(T // 128, defer=True)
        hsel[0] ^= 1
        return fin

    load_tile_input(1)
    layer0_mixer(128, first_real=False, is_halo=True)
    out_proj(128, s_out0.ap(), "s_out0", 1)
    ffn(128, 0, 2)
    layer1_mixer(128, is_halo=True)
    hsel[0] ^= 1
    load_tile_input(T // 128)
    for it in range(NT):
        layer0_mixer(T, first_real=(it == 0), is_halo=False)
        out_proj(T, s_out0.ap(), "s_out0", 1)
        ffn(T, 0, 2)
        if it + 1 < NT:
            issue_x(xblk[0] + 2)
        layer1_mixer(T, is_halo=False)
        out_proj(T, s_out1.ap(), "s_out1", 3)
        ffn(T, 1, 4, final=True, mid_hook=(prefetch_next if it + 1 < NT else None))
        final_out(T, it * T)
        hsel[0] ^= 1

    P.finalize()
    finals = [(("d", "ost0"), P.dma_tot.get("ost0", 0)), (("d", "ost1"), P.dma_tot.get("ost1", 0))]
    with nc.Block() as block:
        @block.sync
        def _(e):
            P.emit("sp", e, esem, dsems)

        @block.tensor
        def _(e):
            P.emit("pe", e, esem, dsems)

        @block.scalar
        def _(e):
            P.emit("act", e, esem, dsems, final_waits=finals)

        @block.vector
        def _(e):
            P.emit("dve", e, esem, dsems)

        @block.gpsimd
        def _(e):
            P.emit("pool", e, esem, dsems)
    es.close()
    return nc


def _pool_mats(first):
    pc = np.zeros((128, 4, 128), np.float32)
    pp = np.zeros((128, 4, 128), np.float32)
    for g, win in enumerate(POOL_WINDOWS):
        for t in range(128):
            cnt = min(t + 1, win) if first else win
            for k in range(win):
                tp = t - k
                if tp >= 0:
                    pc[tp, g, t] += 1.0 / cnt
                elif not first:
                    pp[128 + tp, g, t] += 1.0 / cnt
            pc[t, g, t] -= 1.0
    return pc.reshape(128, 512), pp.reshape(128, 512)


_NC_CACHE = {}


def kernel(x, even_w_in, even_w_out, a_w_s, a_b_s, a_ln_g, a_ln_b, b_w_pool, b_scale,
           odd_w_in, odd_w_out, c_w_dw, c_b_dw, c_ln_g, c_ln_b, d_w_dw,
           norm_mix_g, norm_ffn_g, ffn_w_gate, ffn_w_up, ffn_w_down, final_norm_g):
    f = np.float32
    x = np.asarray(x, f)
    B, S, _ = x.shape
    cps = NCORES // B
    tpc = S // cps
    if tpc not in _NC_CACHE:
        _NC_CACHE[tpc] = build_nc(tpc)
    nc = _NC_CACHE[tpc]

    def chunkvec(v):
        return np.ascontiguousarray(np.asarray(v, f).reshape(-1, 128).T)

    gvec = np.concatenate([chunkvec(norm_mix_g[0]), chunkvec(norm_ffn_g[0]), chunkvec(norm_mix_g[1]),
                           chunkvec(norm_ffn_g[1]), chunkvec(final_norm_g)], axis=1)
    wsT = np.ascontiguousarray(np.transpose(np.asarray(a_w_s, f)[0], (2, 0, 1)).reshape(128, 512))
    pos = np.arange(128)
    mask = (pos[:, None] // 64) >= (pos[None, :] // 64)
    maskT = np.ascontiguousarray(mask.T.astype(f))
    bsbc = np.ascontiguousarray(np.broadcast_to(np.asarray(a_b_s, f)[0].reshape(1, 512), (128, 512)))
    alngc = chunkvec(np.asarray(a_ln_g, f)[0])
    alnbc = chunkvec(np.asarray(a_ln_b, f)[0])
    wpool = np.ascontiguousarray(np.transpose(np.asarray(b_w_pool, f)[0], (1, 0, 2)).reshape(128, 512))
    bscale = chunkvec(np.asarray(b_scale, f)[0])
    pgen, pprev = _pool_mats(False)
    pfirst_seq, _ = _pool_mats(True)
    cw = np.ascontiguousarray(np.transpose(np.asarray(c_w_dw, f)[0].reshape(CK, 4, 128), (2, 1, 0)).reshape(128, 4 * CK))
    dw = np.ascontiguousarray(np.transpose(np.asarray(d_w_dw, f)[0].reshape(DK, 4, 128), (2, 1, 0)).reshape(128, 4 * DK))
    cvec = np.concatenate([chunkvec(np.asarray(c_b_dw, f)[0]), chunkvec(np.asarray(c_ln_g, f)[0]),
                           chunkvec(np.asarray(c_ln_b, f)[0])], axis=1)
    ident = np.eye(128, dtype=f)
    shared = {
        "w_in0": np.ascontiguousarray(np.asarray(even_w_in, f)[0]),
        "w_out0": np.ascontiguousarray(np.asarray(even_w_out, f)[0]),
        "w_in1": np.ascontiguousarray(np.asarray(odd_w_in, f)[0]),
        "w_out1": np.ascontiguousarray(np.asarray(odd_w_out, f)[0]),
        "w_gate": np.ascontiguousarray(np.asarray(ffn_w_gate, f)),
        "w_up": np.ascontiguousarray(np.asarray(ffn_w_up, f)),
        "w_down": np.ascontiguousarray(np.asarray(ffn_w_down, f)),
        "gvec": np.ascontiguousarray(gvec), "wsT": wsT, "maskT": maskT, "bsbc": bsbc, "alngc": alngc, "alnbc": alnbc,
        "wpool": wpool, "bscale": bscale, "pgen": pgen, "pprev": pprev, "cw": cw, "dw": dw,
        "cvec": np.ascontiguousarray(cvec), "ident": ident,
    }
    in_maps = []
    for core in range(NCORES):
        b = core // cps
        part = core % cps
        t0 = part * tpc
        xc = np.zeros((128 + tpc, D), f)
        if part > 0:
            xc[0:128] = x[b, t0 - 128:t0]
        xc[128:] = x[b, t0:t0 + tpc]
        m = dict(shared)
        m["x"] = xc
        m["pfirst"] = pfirst_seq if part == 0 else pgen
        in_maps.append(m)
    res = run_bass_kernel_spmd(nc, in_maps, core_ids=list(range(NCORES)))
    out = np.empty((B, S, D), f)
    for core in range(NCORES):
        b = core // cps
        part = core % cps
        out[b, part * tpc:(part + 1) * tpc] = res.results[core]["out"]
    return out
```

```python
import numpy as np
from contextlib import ExitStack
import concourse.bass as bass
import concourse.mybir as mybir
from concourse.bass_utils import run_bass_kernel_spmd

F32 = mybir.dt.float32
BF16 = mybir.dt.bfloat16
AF = mybir.ActivationFunctionType
ALU = mybir.AluOpType

NCORES = 8
BATCH = 4
SEQ = 8192
D = 1024
DFF = 2816
KC = D // 128
FC = DFF // 128
T = 512
EPS = 1e-6
POOL_WINDOWS = (2, 4, 8, 16)
CK = 31
DK = 3
SLOT = 4096
NSLOT = 4


class Op:
    __slots__ = ("eng", "fn", "is_dma", "dsem", "dcount", "deps", "signal", "sigidx", "idx", "rawwait")


class Prog:
    ENG = ("pe", "act", "dve", "pool", "sp")

    def __init__(self):
        self.ops = []
        self.last_w = {}
        self.readers = {}
        self.dma_tot = {}
        self.group_sems = set()

    def _add(self, eng, fn, reads, writes, is_dma=False, dsem=None):
        op = Op()
        op.eng, op.fn, op.is_dma, op.dsem = eng, fn, is_dma, dsem
        op.idx = len(self.ops)
        op.rawwait = None
        op.signal = False
        op.sigidx = None
        op.dcount = None
        if is_dma:
            self.dma_tot[dsem] = self.dma_tot.get(dsem, 0) + 16
            op.dcount = self.dma_tot[dsem]
        deps = {}
        rd = [k for k in reads]
        wr = [k for k in writes]
        for k in rd:
            if isinstance(k, tuple) and k[0] == "ps":
                wr.append(k)
        rd = [k for k in rd if not (isinstance(k, tuple) and k[0] == "ps")]
        for k in rd:
            w = self.last_w.get(k)
            if w is not None:
                deps[w.idx] = ("raw", w)
        for k in wr:
            w = self.last_w.get(k)
            if w is not None and w.idx not in deps:
                deps[w.idx] = ("raw" if (isinstance(k, tuple) and k[0] == "ps") else "waw", w)
            for r in self.readers.get(k, ()):
                if r.idx not in deps:
                    deps[r.idx] = ("war", r)
        for k in rd:
            self.readers.setdefault(k, []).append(op)
        for k in wr:
            self.last_w[k] = op
            self.readers[k] = []
        op.deps = list(deps.values())
        self.ops.append(op)
        return op

    def pe(self, fn, reads=(), writes=()):
        return self._add("pe", fn, reads, writes)

    def act(self, fn, reads=(), writes=()):
        return self._add("act", fn, reads, writes)

    def dve(self, fn, reads=(), writes=()):
        return self._add("dve", fn, reads, writes)

    def pool(self, fn, reads=(), writes=()):
        return self._add("pool", fn, reads, writes)

    def dma(self, eng, fn, dsem, reads=(), writes=()):
        return self._add(eng, fn, reads, writes, is_dma=True, dsem=dsem)

    def raw_wait(self, eng, dsem, value):
        op = self._add(eng, None, (), ())
        op.rawwait = (dsem, value)
        return op

    def finalize(self):
        need = []
        for op in self.ops:
            lst = []
            best = {}
            for kind, d in op.deps:
                if d.is_dma:
                    lst.append(d)
                    continue
                if d.eng == op.eng:
                    if op.eng == "pe":
                        continue
                    if kind != "raw":
                        continue
                if d.eng not in best or d.idx > best[d.eng].idx:
                    best[d.eng] = d
            for d in best.values():
                lst.append(d)
                d.signal = True
            need.append(lst)
        cnt = {e: 0 for e in self.ENG}
        for op in self.ops:
            if op.signal and not op.is_dma:
                cnt[op.eng] += 1
                op.sigidx = cnt[op.eng]
        self.need = need

    def emit(self, eng_name, eng, esem, dsems, final_waits=()):
        waited = {}
        for op, lst in zip(self.ops, self.need):
            if op.eng != eng_name:
                continue
            req = {}
            for d in lst:
                if d.is_dma:
                    key = ("d", d.dsem)
                    val = self.dma_tot[d.dsem] if d.dsem in self.group_sems else d.dcount
                else:
                    key = ("e", d.eng)
                    val = d.sigidx
                if val > req.get(key, 0):
                    req[key] = val
            for key, val in req.items():
                if waited.get(key, 0) >= val:
                    continue
                waited[key] = val
                sem = dsems[key[1]] if key[0] == "d" else esem[key[1]]
                eng.wait_ge(sem, val)
            if op.rawwait is not None:
                eng.wait_ge(dsems[op.rawwait[0]], op.rawwait[1])
                continue
            ins = op.fn(eng)
            if op.is_dma:
                ins.then_inc(dsems[op.dsem], 16)
            elif op.signal:
                ins.then_inc(esem[op.eng], 1)
        for key, val in final_waits:
            sem = dsems[key[1]] if key[0] == "d" else esem[key[1]]
            eng.wait_ge(sem, val)


def build_nc(tpc):
    assert tpc % T == 0
    NT = tpc // T
    nrows = 128 + tpc
    nc = bass.Bass("TRN2", target_bir_lowering=False)

    def din(name, shape, dt=F32):
        return nc.dram_tensor(name, list(shape), dt, kind="ExternalInput")

    x_d = din("x", [nrows, D])
    w_in0_d = din("w_in0", [D, 1536])
    w_out0_d = din("w_out0", [D, D])
    w_in1_d = din("w_in1", [D, 2560])
    w_out1_d = din("w_out1", [D, D])
    w_gate_d = din("w_gate", [2, D, DFF])
    w_up_d = din("w_up", [2, D, DFF])
    w_down_d = din("w_down", [2, DFF, D])
    gvec_d = din("gvec", [128, 5 * KC])
    wsT_d = din("wsT", [128, 512])
    maskT_d = din("maskT", [128, 128])
    alngc_d = din("alngc", [128, 4])
    alnbc_d = din("alnbc", [128, 4])
    bsbc_d = din("bsbc", [128, 512])
    wpool_d = din("wpool", [128, 512])
    bscale_d = din("bscale", [128, 4])
    pgen_d = din("pgen", [128, 512])
    pfirst_d = din("pfirst", [128, 512])
    pprev_d = din("pprev", [128, 512])
    cw_d = din("cw", [128, 4 * CK])
    dw_d = din("dw", [128, 4 * DK])
    cvec_d = din("cvec", [128, 12])
    ident_d = din("ident", [128, 128])
    out_d = nc.dram_tensor("out", [tpc, D], F32, kind="ExternalOutput")

    s_in0, s_out0, s_in1, s_out1 = w_in0_d, w_out0_d, w_in1_d, w_out1_d
    s_gate, s_up, s_down = w_gate_d, w_up_d, w_down_d

    P = Prog()
    es = ExitStack()

    def sb(name, shape, dt):
        return es.enter_context(nc.sbuf_tensor("sb_" + name, list(shape), dt))

    ident = sb("ident", [128, 128], F32)
    ones = sb("ones", [128, 128], BF16)
    gvec = sb("gvec", [128, 5 * KC], F32)
    wsT = sb("wsTb", [128, 4, 128], BF16)
    maskT = sb("maskT", [128, 128], F32)
    alngc = sb("alngc", [128, 4], F32)
    alnbc = sb("alnbc", [128, 4], F32)
    Bc = sb("Bc", [128, 4, 128], F32)
    ones_f = sb("ones_f", [128, 128], F32)
    wpool = sb("wpoolb", [128, 4, 128], BF16)
    bscale = sb("bscale", [128, 4], F32)
    pgen = sb("pgenb", [128, 4, 128], BF16)
    pfirst = sb("pfirstb", [128, 4, 128], BF16)
    pprev = sb("pprevb", [128, 4, 128], BF16)
    cw = sb("cw", [128, 4 * CK], F32)
    dw = sb("dw", [128, 4 * DK], F32)
    cvec = sb("cvec", [128, 12], F32)
    diagC = sb("diagC", [128, 4 * CK, 128], BF16)
    diagD = sb("diagD", [128, 4 * DK, 128], BF16)
    epsb = sb("epsb", [128, 1], F32)
    warm = sb("warm", [128, 2], F32)

    xin = [sb(f"xin{i}", [128, D], F32) for i in range(2)]
    ostage = xin
    hT2 = [sb("hTa", [128, KC, T], F32), sb("hTb", [128, KC, T], F32)]
    hsel = [0]

    def H():
        return hT2[hsel[0]]

    def kH(c):
        return ("hT", hsel[0], c)
    hn = sb("hn", [128, KC, T], BF16)
    mixT = sb("mixT", [128, KC, T], BF16)
    actT = sb("actT", [128, FC, T], BF16)
    hc = sb("hc", [128, 4, CK - 1 + T], BF16)
    pb = sb("pb", [128, 4, DK - 1 + T], BF16)
    zbt = [sb(f"zbt{i}", [128, 512], BF16) for i in range(3)]
    NF = 15
    Fall = sb("Fall", [128, NF, T], F32)
    Fp = [Fall[:, i, :] for i in range(NF)]
    uT4 = Fall[:, 0:4, :]
    sd_t = sb("sd_t", [128, T], F32)
    rstd_t = sb("rstd_t", [128, T], F32)
    stats = sb("stats", [128, 24], F32)
    mv = sb("mv", [128, 8], F32)
    mvs = sb("mvs", [128, 4], F32)
    mvr = sb("mvr", [128, 4], F32)
    mvn = sb("mvn", [128, 4], F32)
    wring = [sb(f"wr{i}", [128, SLOT], BF16) for i in range(NSLOT)]
    psb = [es.enter_context(nc.psum_tensor(f"ps{i}", [128, 512], F32)) for i in range(8)]

    def kF(i):
        return ("F", i)

    def kA(i):
        return ("actT", i)

    bank_ctr = [0]

    reserved = set()

    def next_bank():
        while True:
            b = bank_ctr[0] % 8
            bank_ctr[0] += 1
            if b not in reserved:
                return b

    def reserve_bank():
        b = next_bank()
        reserved.add(b)
        return b

    def release_bank(b):
        reserved.discard(b)

    esem = {e: es.enter_context(nc.semaphore(f"sem_{e}")) for e in ("pe", "act", "dve", "pool")}
    dsem_names = ["c", "xin0", "xin1", "ost0", "ost1"] + [f"w{i}" for i in range(NSLOT)] + \
                 ["cv0", "cv1", "cv2"]
    dsems = {n: es.enter_context(nc.semaphore(f"dsem_{n}")) for n in dsem_names}
    P.group_sems = {"c"}

    def cload(dst_ap, src_ap, key):
        P.dma("sp", lambda e, d=dst_ap, s=src_ap: e.dma_start(out=d, in_=s), "c", writes=[key])

    cload(ident[:], ident_d.ap(), "ident")
    cload(gvec[:], gvec_d.ap(), "gvec")
    cload(Fp[0][:], wsT_d.ap(), kF(0))
    cload(maskT[:], maskT_d.ap(), "maskT")
    cload(Fp[5], bsbc_d.ap(), kF(5))
    cload(alngc[:], alngc_d.ap(), "alngc")
    cload(alnbc[:], alnbc_d.ap(), "alnbc")
    cload(Fp[1][:], wpool_d.ap(), kF(1))
    cload(bscale[:], bscale_d.ap(), "bscale")
    cload(Fp[2][:], pgen_d.ap(), kF(2))
    cload(Fp[3][:], pfirst_d.ap(), kF(3))
    cload(Fp[4][:], pprev_d.ap(), kF(4))
    cload(cw[:], cw_d.ap(), "cw")
    cload(dw[:], dw_d.ap(), "dw")
    cload(cvec[:], cvec_d.ap(), "cvec")

    P.dve(lambda e: e.memset(ones[:], 1.0), writes=["ones"])
    P.dve(lambda e: e.memset(epsb[:], EPS), writes=["epsb"])
    P.dve(lambda e: e.memset(warm[:], 1.0), writes=["warm0"])
    for h in range(4):
        P.dve(lambda e, h=h: e.tensor_tensor(out=wsT[:, h, :], in0=Fp[0][:, h * 128:(h + 1) * 128],
                                             in1=maskT[:], op=ALU.mult),
              reads=[kF(0), "maskT"], writes=["wsT"])
    P.dve(lambda e: e.tensor_copy(out=wpool[:].rearrange("p g d -> p (g d)"), in_=Fp[1][:]),
          reads=[kF(1)], writes=["wpool"])
    P.dve(lambda e: e.tensor_copy(out=pgen[:].rearrange("p g d -> p (g d)"), in_=Fp[2][:]),
          reads=[kF(2)], writes=["pgen"])
    P.dve(lambda e: e.tensor_copy(out=pfirst[:].rearrange("p g d -> p (g d)"), in_=Fp[3][:]),
          reads=[kF(3)], writes=["pfirst"])
    P.dve(lambda e: e.tensor_copy(out=pprev[:].rearrange("p g d -> p (g d)"), in_=Fp[4][:]),
          reads=[kF(4)], writes=["pprev"])
    P.dve(lambda e: e.memset(ones_f[:], 1.0), writes=["ones_f"])
    for h in range(4):
        P.dve(lambda e, h=h: e.tensor_tensor(out=Fp[6][:, h * 128:(h + 1) * 128], in0=Fp[0][:, h * 128:(h + 1) * 128],
                                             in1=maskT[:], op=ALU.mult),
              reads=[kF(0), "maskT"], writes=[kF(6)])
    bkc = next_bank()
    P.pe(lambda e: e.matmul(psb[bkc][:], ones_f[:], Fp[6], start=True, stop=True),
         reads=["ones_f", kF(6)], writes=[("ps", bkc)])
    for h in range(4):
        P.dve(lambda e, h=h: e.scalar_tensor_tensor(out=Bc[:, h, :], in0=psb[bkc][:, h * 128:(h + 1) * 128],
                                                    scalar=alnbc[:, h:h + 1], in1=Fp[5][:, h * 128:(h + 1) * 128],
                                                    op0=ALU.mult, op1=ALU.add),
              reads=[("ps", bkc), "alnbc", kF(5)], writes=["Bc"])
    for i in range(4 * CK):
        P.dve(lambda e, i=i: e.tensor_scalar(out=diagC[:, i, :], in0=ident[:], scalar1=cw[:, i:i + 1],
                                             scalar2=None, op0=ALU.mult),
              reads=["ident", "cw"], writes=["diagC"])
    for i in range(4 * DK):
        P.dve(lambda e, i=i: e.tensor_scalar(out=diagD[:, i, :], in0=ident[:], scalar1=dw[:, i:i + 1],
                                             scalar2=None, op0=ALU.mult),
              reads=["ident", "dw"], writes=["diagD"])

    slot_ctr = [0]

    def wload(src_ap, ncols_elems, view_fn, src_key):
        s = slot_ctr[0] % NSLOT
        slot_ctr[0] += 1
        dst = view_fn(wring[s])
        P.dma("pool", lambda e, d=dst, a=src_ap: e.dma_start(out=d, in_=a), f"w{s}",
              reads=["ident", "cvec"], writes=[("w", s)])
        return s, dst

    def wunit_kn(src2d, col0, ncols, src_key):
        src = src2d[:, col0:col0 + ncols].rearrange("(k p) n -> p k n", p=128)
        return wload(src, 8 * ncols,
                     lambda w: w[:, 0:8 * ncols].rearrange("p (k n) -> p k n", k=8), src_key)

    def wunit_down(src2d, hf, q, src_key):
        src = src2d[hf * 11 * 128:(hf + 1) * 11 * 128, q * 256:(q + 1) * 256].rearrange("(i p) n -> p i n", p=128)
        return wload(src, 11 * 256,
                     lambda w: w[:, 0:11 * 256].rearrange("p (i n) -> p i n", i=11), src_key)

    def mm(out, lhsT, rhs, start, stop, reads, writes):
        P.pe(lambda e: e.matmul(out, lhsT, rhs, start=start, stop=stop), reads=reads, writes=writes)

    def load_x_block(blk_global, slot):
        r0 = blk_global * 128
        P.dma("sp", lambda e: e.dma_start(out=xin[slot][:], in_=x_d.ap()[r0:r0 + 128, :]), f"xin{slot}",
              writes=[("xin", slot)])

    def input_block(b, slot):
        cols = slice(b * 128, (b + 1) * 128)
        for half in range(2):
            bk = next_bank()
            for cc in range(4):
                c = half * 4 + cc
                P.pe(lambda e, c=c, cc=cc, bk=bk: e.transpose(psb[bk][:, cc * 128:(cc + 1) * 128],
                                                             xin[slot][:, c * 128:(c + 1) * 128], ident[:]),
                     reads=[("xin", slot), "ident"], writes=[("ps", bk)])
            src = psb[bk][:].rearrange("p (c t) -> p c t", c=4)
            dst = H()[:, half * 4:(half + 1) * 4, cols]
            keys = [kH(half * 4 + cc) for cc in range(4)]
            if half == 0:
                P.act(lambda e, s=src, d=dst: e.activation(out=d, in_=s, func=AF.Copy),
                      reads=[("ps", bk)], writes=keys)
            else:
                P.dve(lambda e, s=src, d=dst: e.tensor_copy(out=d, in_=s), reads=[("ps", bk)], writes=keys)

    def input_square(b):
        cols = slice(b * 128, (b + 1) * 128)
        P.act(lambda e, h=H(): e.activation(out=hn[:, 0:8, cols], in_=h[:, 0:8, cols], func=AF.Square),
              reads=[kH(c) for c in range(KC)], writes=[("hn", c) for c in range(KC)])

    def input_ssq(b, ssq_bank, first, last):
        cols = slice(b * 128, (b + 1) * 128)
        for c in range(KC):
            mm(psb[ssq_bank][:, cols], ones[:], hn[:, c, cols], first and c == 0, last and c == KC - 1,
               ["ones", ("hn", c)], [("ps", ssq_bank)])

    def recip(Tt):
        P.dve(lambda e: e.reciprocal(out=rstd_t[:, 0:Tt], in_=sd_t[:, 0:Tt]), reads=["sd"], writes=["rstd"])

    def warm_sqrt():
        P.act(lambda e: e.activation(out=warm[:, 1:2], in_=warm[:, 0:1], func=AF.Sqrt), reads=["warm0"], writes=["warm1"])

    def norm_finish(Tt, gidx, bk, out_bf16=True, out_tiles=None):
        P.act(lambda e: e.activation(out=sd_t[:, 0:Tt], in_=psb[bk][:, 0:Tt], func=AF.Sqrt, scale=1.0 / D,
                                     bias=epsb[:, 0:1]),
              reads=[("ps", bk), "epsb"], writes=["sd"])
        release_bank(bk)
        recip(Tt)
        for c in range(KC):
            g_ap = gvec[:, gidx * KC + c:gidx * KC + c + 1]
            if out_bf16:
                P.dve(lambda e, c=c, g=g_ap, h=H(): e.scalar_tensor_tensor(out=hn[:, c, 0:Tt], in0=h[:, c, 0:Tt], scalar=g,
                                                                           in1=rstd_t[:, 0:Tt], op0=ALU.mult, op1=ALU.mult),
                      reads=[kH(c), "rstd", "gvec"], writes=[("hn", c)])
            else:
                fi = out_tiles[c]
                P.dve(lambda e, c=c, g=g_ap, fi=fi, h=H(): e.scalar_tensor_tensor(out=Fp[fi][:, 0:Tt], in0=h[:, c, 0:Tt],
                                                                                  scalar=g, in1=rstd_t[:, 0:Tt],
                                                                                  op0=ALU.mult, op1=ALU.mult),
                      reads=[kH(c), "rstd", "gvec"], writes=[kF(fi)])

    class NormAcc:
        def __init__(self, Tt, sqbuf, sqkey):
            self.Tt, self.sqbuf, self.sqkey = Tt, sqbuf, sqkey
            self.bank = reserve_bank()
            self.pending = []
            self.n_mm = 0
            self.warmed = False

        def chunk_done(self, c):
            Tt = self.Tt
            P.act(lambda e, c=c, h=H(): e.activation(out=self.sqbuf[:, c, 0:Tt], in_=h[:, c, 0:Tt], func=AF.Square),
                  reads=[kH(c)], writes=[self.sqkey(c)])
            self.pending.append(c)
            if not self.warmed:
                warm_sqrt()
                self.warmed = True

        def flush(self, keep=0):
            while len(self.pending) > keep:
                c = self.pending.pop(0)
                mm(psb[self.bank][:, 0:self.Tt], ones[:], self.sqbuf[:, c, 0:self.Tt], self.n_mm == 0,
                   self.n_mm == KC - 1, ["ones", self.sqkey(c)], [("ps", self.bank)])
                self.n_mm += 1

    HN_KEYS = [("hn", c) for c in range(KC)]

    def proj_fm(Tt, wview, wslot, oc, bk, rhs_tile, rhs_keys, nk=KC):
        for k in range(nk):
            mm(psb[bk][:, 0:Tt], wview[:, k, oc * 128:(oc + 1) * 128], rhs_tile[:, k, 0:Tt], k == 0, k == nk - 1,
               [("w", wslot), rhs_keys[k]], [("ps", bk)])

    def proj_fm_kouter(Tt, wview, wslot, ocs, bks, rhs_tile, rhs_keys):
        for k in range(KC):
            for oc, bk in zip(ocs, bks):
                mm(psb[bk][:, 0:Tt], wview[:, k, oc * 128:(oc + 1) * 128], rhs_tile[:, k, 0:Tt], k == 0, k == KC - 1,
                   [("w", wslot), rhs_keys[k]], [("ps", bk)])

    def proj_tm(b, wview, wslot, bk, ncols=512):
        for k in range(KC):
            mm(psb[bk][:, 0:ncols], hn[:, k, b * 128:(b + 1) * 128], wview[:, k, 0:ncols], k == 0, k == KC - 1,
               [("w", wslot), ("hn", k)], [("ps", bk)])

    zb_ctr = [0]

    def layer0_mixer(Tt, first_real, is_halo):
        NBt = Tt // 128
        s_u, v_u = wunit_kn(s_in0.ap(), 0, 512, "s_in0")
        bks = [next_bank() for _ in range(4)]
        proj_fm_kouter(Tt, v_u, s_u, range(4), bks, hn, HN_KEYS)
        for oc in range(4):
            bk = bks[oc]
            P.act(lambda e, oc=oc, bk=bk: e.activation(out=uT4[:, oc, 0:Tt], in_=psb[bk][:, 0:Tt],
                                                       func=AF.Gelu_apprx_tanh),
                  reads=[("ps", bk)], writes=[kF(oc)])
        s_v, v_v = wunit_kn(s_in0.ap(), 512, 512, "s_in0")
        for b in range(NBt):
            bk = next_bank()
            proj_tm(b, v_v, s_v, bk)
            vt = 4 + b
            P.act(lambda e, bk=bk, vt=vt: e.activation(out=Fp[vt][:], in_=psb[bk][:], func=AF.Gelu_apprx_tanh),
                  reads=[("ps", bk)], writes=[kF(vt)])
            P.dve(lambda e, vt=vt, b=b: e.bn_stats(out=stats[:, 6 * b:6 * b + 6], in_=Fp[vt][:]),
                  reads=[kF(vt)], writes=[("stats", b)])
            P.dve(lambda e, b=b: e.bn_aggr(out=mv[:, 2 * b:2 * b + 2], in_=stats[:, 6 * b:6 * b + 6]),
                  reads=[("stats", b)], writes=[("mv", b)])
        mvv = mv[:, 0:2 * NBt].rearrange("p (b t) -> p b t", t=2)[:, :, 1:2]
        P.act(lambda e: e.activation(out=mvs[:, 0:NBt].rearrange("p (b o) -> p b o", o=1), in_=mvv, func=AF.Sqrt,
                                     bias=epsb[:, 0:1]),
              reads=[("mv", b) for b in range(NBt)] + ["epsb"], writes=["mvs"])
        P.dve(lambda e: e.reciprocal(out=mvr[:, 0:NBt], in_=mvs[:, 0:NBt]), reads=["mvs"], writes=["mvr"])
        mvm = mv[:, 0:2 * NBt].rearrange("p (b t) -> p b t", t=2)[:, :, 0:1]
        P.dve(lambda e: e.scalar_tensor_tensor(out=mvn[:, 0:NBt].rearrange("p (b o) -> p b o", o=1), in0=mvm, scalar=-1.0,
                                               in1=mvr[:, 0:NBt].rearrange("p (b o) -> p b o", o=1),
                                               op0=ALU.mult, op1=ALU.mult),
              reads=[("mv", b) for b in range(NBt)] + ["mvr"], writes=["mvn"])
        s_z, v_z = wunit_kn(s_in0.ap(), 1024, 512, "s_in0")

        def pooling(b, cur, prev):
            bk2 = next_bank()
            pc = pfirst if (first_real and b == 0) else pgen
            pck = "pfirst" if (first_real and b == 0) else "pgen"
            for g in range(4):
                mm(psb[bk2][:, g * 128:(g + 1) * 128], zbt[cur][:, g * 128:(g + 1) * 128], pc[:, g, :],
                   g == 0, False if not is_halo else (g == 3), [("zbt", cur), pck], [("ps", bk2)])
            if not is_halo:
                for g in range(4):
                    mm(psb[bk2][:, g * 128:g * 128 + 16], zbt[prev][:, g * 128:(g + 1) * 128], pprev[:, g, 0:16],
                       False, g == 3, [("zbt", prev), "pprev"], [("ps", bk2)])
            P.act(lambda e, bk2=bk2, b=b: e.activation(out=actT[:, 8:12, b * 128:(b + 1) * 128],
                                                       in_=psb[bk2][:].rearrange("p (g t) -> p g t", g=4),
                                                       func=AF.Copy),
                  reads=[("ps", bk2)], writes=[kA(8), kA(9), kA(10), kA(11)])

        pend = None
        for b in range(NBt):
            bk = next_bank()
            proj_tm(b, v_z, s_z, bk)
            cur = zb_ctr[0] % 3
            prev = (zb_ctr[0] - 1) % 3
            zb_ctr[0] += 1
            P.act(lambda e, bk=bk, cur=cur: e.activation(out=zbt[cur][:], in_=psb[bk][:], func=AF.Copy),
                  reads=[("ps", bk)], writes=[("zbt", cur)])
            if pend is not None:
                pooling(*pend)
            pend = (b, cur, prev)
        pooling(*pend)
        for b in range(NBt):
            vt = 4 + b
            vn_i = 16 + (b % 2)
            P.act(lambda e, vt=vt, b=b, vn_i=vn_i: e.activation(out=actT[:, vn_i, :], in_=Fp[vt][:], func=AF.Identity,
                                                                scale=mvr[:, b:b + 1], bias=mvn[:, b:b + 1]),
                  reads=[kF(vt), "mvr", "mvn"], writes=[kA(vn_i)])
            bk2 = next_bank()
            for h in range(4):
                mm(psb[bk2][:, h * 128:(h + 1) * 128], actT[:, vn_i, h * 128:(h + 1) * 128], wsT[:, h, :],
                   h == 0, h == 3, [kA(vn_i), "wsT"], [("ps", bk2)])
            svt = 8 + (b % 2)
            for h in range(4):
                P.dve(lambda e, h=h, bk2=bk2, svt=svt: e.scalar_tensor_tensor(
                    out=Fp[svt][:, h * 128:(h + 1) * 128], in0=psb[bk2][:, h * 128:(h + 1) * 128],
                    scalar=alngc[:, h:h + 1], in1=Bc[:, h, :], op0=ALU.mult, op1=ALU.add),
                    reads=[("ps", bk2), "alngc", "Bc"], writes=[kF(svt)])
            P.dve(lambda e, svt=svt, b=b: e.tensor_tensor(out=mixT[:, 0:4, b * 128:(b + 1) * 128],
                                                          in0=Fp[svt].rearrange("p (h t) -> p h t", h=4),
                                                          in1=uT4[:, :, b * 128:(b + 1) * 128], op=ALU.mult),
                  reads=[kF(svt), kF(0), kF(1), kF(2), kF(3)], writes=[("mixT", h) for h in range(4)])
        for g in range(4):
            bk = next_bank()
            mm(psb[bk][:, 0:Tt], wpool[:, g, :], actT[:, 8 + g, 0:Tt], True, True, ["wpool", kA(8 + g)], [("ps", bk)])
            P.act(lambda e, g=g, bk=bk: e.activation(out=mixT[:, 4 + g, 0:Tt], in_=psb[bk][:, 0:Tt], func=AF.Copy,
                                                     scale=bscale[:, g:g + 1]),
                  reads=[("ps", bk), "bscale"], writes=[("mixT", 4 + g)])

    MIX_KEYS = [("mixT", c) for c in range(KC)]

    def out_proj(Tt, s_out, key, gidx_next):
        na = NormAcc(Tt, actT, kA)
        for j in range(2):
            s_w, v_w = wunit_kn(s_out, j * 512, 512, key)
            for oc in range(4):
                c = j * 4 + oc
                bk = next_bank()
                proj_fm(Tt, v_w, s_w, oc, bk, mixT, MIX_KEYS)
                P.dve(lambda e, c=c, bk=bk, h=H(): e.tensor_tensor(out=h[:, c, 0:Tt], in0=psb[bk][:, 0:Tt],
                                                                   in1=h[:, c, 0:Tt], op=ALU.add),
                      reads=[("ps", bk), kH(c)], writes=[kH(c)])
                na.chunk_done(c)
                na.flush(keep=2)
        na.flush()
        norm_finish(Tt, gidx_next, na.bank)

    def ffn(Tt, layer, gidx_next, final=False, mid_hook=None):
        sg_ = s_gate.ap()[layer]
        su_ = s_up.ap()[layer]
        sd_ = s_down.ap()[layer]
        for j in range(6):
            ncols = 512 if j < 5 else 256
            s_g, v_g = wunit_kn(sg_, j * 512, ncols, f"s_g{layer}")
            s_u, v_u = wunit_kn(su_, j * 512, ncols, f"s_u{layer}")
            if j == 0:
                bgs = [next_bank() for _ in range(4)]
                proj_fm_kouter(Tt, v_g, s_g, range(4), bgs, hn, HN_KEYS)
            for oc in range(ncols // 128):
                fc = j * 4 + oc
                if j == 0:
                    bg = bgs[oc]
                else:
                    bg = next_bank()
                    proj_fm(Tt, v_g, s_g, oc, bg, hn, HN_KEYS)
                bu = next_bank()
                proj_fm(Tt, v_u, s_u, oc, bu, hn, HN_KEYS)
                st = 8 + (fc % 2)
                P.act(lambda e, bg=bg, st=st: e.activation(out=Fp[st][:, 0:Tt], in_=psb[bg][:, 0:Tt], func=AF.Silu),
                      reads=[("ps", bg)], writes=[kF(st)])
                P.dve(lambda e, bu=bu, st=st, fc=fc: e.tensor_tensor(out=actT[:, fc, 0:Tt], in0=psb[bu][:, 0:Tt],
                                                                     in1=Fp[st][:, 0:Tt], op=ALU.mult),
                      reads=[("ps", bu), kF(st)], writes=[kA(fc)])
        late = list(mid_hook()) if mid_hook is not None else []
        for _ in range(2):
            if len(late) > 1:
                late.pop(0)()
        na = NormAcc(Tt, mixT, lambda c: ("mixT", c))
        for q in range(4):
            b0 = next_bank()
            b1 = next_bank()
            bks = (b0, b1)
            for hf in range(2):
                s_d, v_d = wunit_down(sd_, hf, q, f"s_d{layer}")
                for i in range(11):
                    fc = hf * 11 + i
                    for o2 in range(2):
                        mm(psb[bks[o2]][:, 0:Tt], v_d[:, i, o2 * 128:(o2 + 1) * 128], actT[:, fc, 0:Tt],
                           fc == 0, fc == FC - 1, [("w", s_d), kA(fc)], [("ps", bks[o2])])
            na.flush()
            if late:
                late.pop(0)()
            for o2 in range(2):
                c = q * 2 + o2
                P.dve(lambda e, c=c, bk=bks[o2], h=H(): e.tensor_tensor(out=h[:, c, 0:Tt], in0=psb[bk][:, 0:Tt],
                                                                        in1=h[:, c, 0:Tt], op=ALU.add),
                      reads=[("ps", bks[o2]), kH(c)], writes=[kH(c)])
                na.chunk_done(c)
        while late:
            late.pop(0)()
        na.flush()
        if final:
            norm_finish(Tt, gidx_next, na.bank, out_bf16=False, out_tiles=list(range(8)))
        else:
            norm_finish(Tt, gidx_next, na.bank)

    def layer1_mixer(Tt, is_halo):
        H0 = CK - 1
        Q0 = DK - 1
        s_a, v_a = wunit_kn(s_in1.ap(), 0, 512, "s_in1")
        s_g, v_g = wunit_kn(s_in1.ap(), 512, 512, "s_in1")
        bas = [next_bank() for _ in range(4)]
        proj_fm_kouter(Tt, v_a, s_a, range(4), bas, hn, HN_KEYS)
        for c in range(4):
            ba = bas[c]
            bg = next_bank()
            proj_fm(Tt, v_g, s_g, c, bg, hn, HN_KEYS)
            st = c % 2
            P.act(lambda e, bg=bg, st=st: e.activation(out=Fp[st][:, 0:Tt], in_=psb[bg][:, 0:Tt], func=AF.Sigmoid),
                  reads=[("ps", bg)], writes=[kF(st)])
            P.dve(lambda e, ba=ba, st=st, c=c: e.tensor_tensor(out=hc[:, c, H0:H0 + Tt], in0=psb[ba][:, 0:Tt],
                                                               in1=Fp[st][:, 0:Tt], op=ALU.mult),
                  reads=[("ps", ba), kF(st)], writes=["hc"])
        if not is_halo:
            for c in range(4):
                bk = next_bank()
                for j in range(CK):
                    mm(psb[bk][:, 0:Tt], diagC[:, c * CK + j, :], hc[:, c, j:j + Tt], j == 0, j == CK - 1,
                       ["diagC", "hc"], [("ps", bk)])
                P.act(lambda e, c=c, bk=bk: e.activation(out=actT[:, 8 + c, 0:Tt], in_=psb[bk][:, 0:Tt],
                                                         func=AF.Identity, bias=cvec[:, c:c + 1]),
                      reads=[("ps", bk), "cvec"], writes=[kA(8 + c)])
                P.act(lambda e, c=c, bk=bk: e.activation(out=actT[:, 12 + c, 0:Tt], in_=psb[bk][:, 0:Tt],
                                                         func=AF.Square, bias=cvec[:, c:c + 1]),
                      reads=[("ps", bk), "cvec"], writes=[kA(12 + c)])
                P.act(lambda e, c=c, bk=bk: e.activation(out=Fp[6 + c][:, 0:Tt], in_=psb[bk][:, 0:Tt],
                                                         func=AF.Identity, bias=cvec[:, c:c + 1]),
                      reads=[("ps", bk), "cvec"], writes=[kF(6 + c)])
            b1 = next_bank()
            for c in range(4):
                mm(psb[b1][:, 0:Tt], ones[:], actT[:, 8 + c, 0:Tt], c == 0, c == 3, ["ones", kA(8 + c)], [("ps", b1)])
            b2 = next_bank()
            for c in range(4):
                mm(psb[b2][:, 0:Tt], ones[:], actT[:, 12 + c, 0:Tt], c == 0, c == 3, ["ones", kA(12 + c)], [("ps", b2)])
            P.dve(lambda e: e.tensor_scalar(out=Fp[10][:, 0:Tt], in0=psb[b1][:, 0:Tt], scalar1=1.0 / 512, scalar2=None,
                                            op0=ALU.mult), reads=[("ps", b1)], writes=[kF(10)])
            P.dve(lambda e: e.tensor_tensor(out=Fp[11][:, 0:Tt], in0=Fp[10][:, 0:Tt], in1=Fp[10][:, 0:Tt], op=ALU.mult),
                  reads=[kF(10)], writes=[kF(11)])
            P.dve(lambda e: e.scalar_tensor_tensor(out=Fp[12][:, 0:Tt], in0=psb[b2][:, 0:Tt], scalar=1.0 / 512,
                                                   in1=Fp[11][:, 0:Tt], op0=ALU.mult, op1=ALU.subtract),
                  reads=[("ps", b2), kF(11)], writes=[kF(12)])
            P.act(lambda e: e.activation(out=sd_t[:, 0:Tt], in_=Fp[12][:, 0:Tt], func=AF.Sqrt, bias=epsb[:, 0:1]),
                  reads=[kF(12), "epsb"], writes=["sd"])
            recip(Tt)
            for c in range(4):
                t1 = 13 + (c % 2)
                P.dve(lambda e, c=c, t1=t1: e.tensor_tensor(out=Fp[t1][:, 0:Tt], in0=Fp[6 + c][:, 0:Tt],
                                                            in1=Fp[10][:, 0:Tt], op=ALU.subtract),
                      reads=[kF(6 + c), kF(10)], writes=[kF(t1)])
                P.dve(lambda e, t1=t1: e.tensor_tensor(out=Fp[t1][:, 0:Tt], in0=Fp[t1][:, 0:Tt], in1=rstd_t[:, 0:Tt],
                                                       op=ALU.mult),
                      reads=[kF(t1), "rstd"], writes=[kF(t1)])
                P.act(lambda e, c=c, t1=t1: e.activation(out=mixT[:, c, 0:Tt], in_=Fp[t1][:, 0:Tt], func=AF.Silu,
                                                         scale=cvec[:, 4 + c:5 + c], bias=cvec[:, 8 + c:9 + c]),
                      reads=[kF(t1), "cvec"], writes=[("mixT", c)])
            s_b, v_b = wunit_kn(s_in1.ap(), 1024, 512, "s_in1")
            for c in range(4):
                bb = next_bank()
                proj_fm(Tt, v_b, s_b, c, bb, hn, HN_KEYS)
                P.act(lambda e, bb=bb, c=c: e.activation(out=Fp[2 + c][:, 0:Tt], in_=psb[bb][:, 0:Tt], func=AF.Copy),
                      reads=[("ps", bb)], writes=[kF(2 + c)])
        s_c, v_c = wunit_kn(s_in1.ap(), 1536, 512, "s_in1")
        s_x, v_x = wunit_kn(s_in1.ap(), 2048, 512, "s_in1")
        for c in range(4):
            bc = next_bank()
            proj_fm(Tt, v_c, s_c, c, bc, hn, HN_KEYS)
            bx = next_bank()
            proj_fm(Tt, v_x, s_x, c, bx, hn, HN_KEYS)
            st = c % 2
            P.act(lambda e, bc=bc, st=st: e.activation(out=Fp[st][:, 0:Tt], in_=psb[bc][:, 0:Tt], func=AF.Copy),
                  reads=[("ps", bc)], writes=[kF(st)])
            P.dve(lambda e, bx=bx, st=st, c=c: e.tensor_tensor(out=pb[:, c, Q0:Q0 + Tt], in0=psb[bx][:, 0:Tt],
                                                               in1=Fp[st][:, 0:Tt], op=ALU.mult),
                  reads=[("ps", bx), kF(st)], writes=["pb"])
        if not is_halo:
            for c in range(4):
                bk = next_bank()
                for j in range(DK):
                    mm(psb[bk][:, 0:Tt], diagD[:, c * DK + j, :], pb[:, c, j:j + Tt], j == 0, j == DK - 1,
                       ["diagD", "pb"], [("ps", bk)])
                P.dve(lambda e, c=c, bk=bk: e.tensor_tensor(out=mixT[:, 4 + c, 0:Tt], in0=psb[bk][:, 0:Tt],
                                                            in1=Fp[2 + c][:, 0:Tt], op=ALU.mult),
                      reads=[("ps", bk), kF(2 + c)], writes=[("mixT", 4 + c)])
        P.act(lambda e: e.activation(out=hc[:, :, 0:H0], in_=hc[:, :, Tt:Tt + H0], func=AF.Copy),
              reads=["hc"], writes=["hc"])
        P.act(lambda e: e.activation(out=pb[:, :, 0:Q0], in_=pb[:, :, Tt:Tt + Q0], func=AF.Copy),
              reads=["pb"], writes=["pb"])

    ost_ctr = [0]

    def final_out(Tt, row0):
        for b in range(Tt // 128):
            slot = ost_ctr[0] % 2
            ost_ctr[0] += 1
            for half in range(2):
                bk = next_bank()
                for cc in range(4):
                    c = half * 4 + cc
                    P.pe(lambda e, c=c, cc=cc, bk=bk, b=b: e.transpose(psb[bk][:, cc * 128:(cc + 1) * 128],
                                                                      Fp[c][:, b * 128:(b + 1) * 128], ident[:]),
                         reads=[kF(c), "ident"], writes=[("ps", bk)])
                dst = ostage[slot][:, half * 512:(half + 1) * 512]
                if half == 0:
                    P.act(lambda e, bk=bk, d=dst: e.activation(out=d, in_=psb[bk][:], func=AF.Copy),
                          reads=[("ps", bk)], writes=[("xin", slot)])
                else:
                    P.dve(lambda e, bk=bk, d=dst: e.tensor_copy(out=d, in_=psb[bk][:]),
                          reads=[("ps", bk)], writes=[("xin", slot)])
            r0 = row0 + b * 128
            P.dma("act", lambda e, slot=slot, r0=r0: e.dma_start(out=out_d.ap()[r0:r0 + 128, :], in_=ostage[slot][:]),
                  f"ost{slot}", reads=[("xin", slot)], writes=["out_dram"])

    xblk = [0]

    xissued = [0]

    def issue_x(upto):
        while xissued[0] < upto and xissued[0] < nrows // 128:
            load_x_block(xissued[0], xissued[0] % 2)
            xissued[0] += 1

    def load_tile_input(NBt, defer=False):
        bank = reserve_bank()
        hs = hsel[0]
        steps = []

        def mk_block(b):
            def step():
                save = hsel[0]
                hsel[0] = hs
                slot = xblk[0] % 2
                issue_x(xblk[0] + 1)
                input_block(b, slot)
                xblk[0] += 1
                if b > 0:
                    input_ssq(b - 1, bank, b - 1 == 0, False)
                input_square(b)
                if b == 0:
                    warm_sqrt()
                hsel[0] = save
            return step

        def finish():
            save = hsel[0]
            hsel[0] = hs
            input_ssq(NBt - 1, bank, NBt == 1, True)
            norm_finish(NBt * 128, 0, bank)
            hsel[0] = save

        for b in range(NBt):
            steps.append(mk_block(b))
        steps.append(finish)
        if defer:
            return steps
        for st in steps:
            st()
        return None

    def prefetch_next():
        hsel[0] ^= 1
        fin = load_tile_input(T // 128, defer=True)
        hsel[0] ^= 1
        return fin

    load_tile_input(1)
    layer0_mixer(128, first_real=False, is_halo=True)
    out_proj(128, s_out0.ap(), "s_out0", 1)
    ffn(128, 0, 2)
    layer1_mixer(128, is_halo=True)
    hsel[0] ^= 1
    load_tile_input(T // 128)
    for it in range(NT):
        layer0_mixer(T, first_real=(it == 0), is_halo=False)
        out_proj(T, s_out0.ap(), "s_out0", 1)
        ffn(T, 0, 2)
        if it + 1 < NT:
            issue_x(xblk[0] + 2)
        layer1_mixer(T, is_halo=False)
        out_proj(T, s_out1.ap(), "s_out1", 3)
        ffn(T, 1, 4, final=True, mid_hook=(prefetch_next if it + 1 < NT else None))
        final_out(T, it * T)
        hsel[0] ^= 1

    P.finalize()
    finals = [(("d", "ost0"), P.dma_tot.get("ost0", 0)), (("d", "ost1"), P.dma_tot.get("ost1", 0))]
    with nc.Block() as block:
        @block.sync
        def _(e):
            P.emit("sp", e, esem, dsems)

        @block.tensor
        def _(e):
            P.emit("pe", e, esem, dsems)

        @block.scalar
        def _(e):
            P.emit("act", e, esem, dsems, final_waits=finals)

        @block.vector
        def _(e):
            P.emit("dve", e, esem, dsems)

        @block.gpsimd
        def _(e):
            P.emit("pool", e, esem, dsems)
    es.close()
    return nc


def _pool_mats(first):
    pc = np.zeros((128, 4, 128), np.float32)
    pp = np.zeros((128, 4, 128), np.float32)
    for g, win in enumerate(POOL_WINDOWS):
        for t in range(128):
            cnt = min(t + 1, win) if first else win
            for k in range(win):
                tp = t - k
                if tp >= 0:
                    pc[tp, g, t] += 1.0 / cnt
                elif not first:
                    pp[128 + tp, g, t] += 1.0 / cnt
            pc[t, g, t] -= 1.0
    return pc.reshape(128, 512), pp.reshape(128, 512)


_NC_CACHE = {}


def kernel(x, even_w_in, even_w_out, a_w_s, a_b_s, a_ln_g, a_ln_b, b_w_pool, b_scale,
           odd_w_in, odd_w_out, c_w_dw, c_b_dw, c_ln_g, c_ln_b, d_w_dw,
           norm_mix_g, norm_ffn_g, ffn_w_gate, ffn_w_up, ffn_w_down, final_norm_g):
    f = np.float32
    x = np.asarray(x, f)
    B, S, _ = x.shape
    cps = NCORES // B
    tpc = S // cps
    if tpc not in _NC_CACHE:
        _NC_CACHE[tpc] = build_nc(tpc)
    nc = _NC_CACHE[tpc]

    def chunkvec(v):
        return np.ascontiguousarray(np.asarray(v, f).reshape(-1, 128).T)

    gvec = np.concatenate([chunkvec(norm_mix_g[0]), chunkvec(norm_ffn_g[0]), chunkvec(norm_mix_g[1]),
                           chunkvec(norm_ffn_g[1]), chunkvec(final_norm_g)], axis=1)
    wsT = np.ascontiguousarray(np.transpose(np.asarray(a_w_s, f)[0], (2, 0, 1)).reshape(128, 512))
    pos = np.arange(128)
    mask = (pos[:, None] // 64) >= (pos[None, :] // 64)
    maskT = np.ascontiguousarray(mask.T.astype(f))
    bsbc = np.ascontiguousarray(np.broadcast_to(np.asarray(a_b_s, f)[0].reshape(1, 512), (128, 512)))
    alngc = chunkvec(np.asarray(a_ln_g, f)[0])
    alnbc = chunkvec(np.asarray(a_ln_b, f)[0])
    wpool = np.ascontiguousarray(np.transpose(np.asarray(b_w_pool, f)[0], (1, 0, 2)).reshape(128, 512))
    bscale = chunkvec(np.asarray(b_scale, f)[0])
    pgen, pprev = _pool_mats(False)
    pfirst_seq, _ = _pool_mats(True)
    cw = np.ascontiguousarray(np.transpose(np.asarray(c_w_dw, f)[0].reshape(CK, 4, 128), (2, 1, 0)).reshape(128, 4 * CK))
    dw = np.ascontiguousarray(np.transpose(np.asarray(d_w_dw, f)[0].reshape(DK, 4, 128), (2, 1, 0)).reshape(128, 4 * DK))
    cvec = np.concatenate([chunkvec(np.asarray(c_b_dw, f)[0]), chunkvec(np.asarray(c_ln_g, f)[0]),
                           chunkvec(np.asarray(c_ln_b, f)[0])], axis=1)
    ident = np.eye(128, dtype=f)
    shared = {
        "w_in0": np.ascontiguousarray(np.asarray(even_w_in, f)[0]),
        "w_out0": np.ascontiguousarray(np.asarray(even_w_out, f)[0]),
        "w_in1": np.ascontiguousarray(np.asarray(odd_w_in, f)[0]),
        "w_out1": np.ascontiguousarray(np.asarray(odd_w_out, f)[0]),
        "w_gate": np.ascontiguousarray(np.asarray(ffn_w_gate, f)),
        "w_up": np.ascontiguousarray(np.asarray(ffn_w_up, f)),
        "w_down": np.ascontiguousarray(np.asarray(ffn_w_down, f)),
        "gvec": np.ascontiguousarray(gvec), "wsT": wsT, "maskT": maskT, "bsbc": bsbc, "alngc": alngc, "alnbc": alnbc,
        "wpool": wpool, "bscale": bscale, "pgen": pgen, "pprev": pprev, "cw": cw, "dw": dw,
        "cvec": np.ascontiguousarray(cvec), "ident": ident,
    }
    in_maps = []
    for core in range(NCORES):
        b = core // cps
        part = core % cps
        t0 = part * tpc
        xc = np.zeros((128 + tpc, D), f)
        if part > 0:
            xc[0:128] = x[b, t0 - 128:t0]
        xc[128:] = x[b, t0:t0 + tpc]
        m = dict(shared)
        m["x"] = xc
        m["pfirst"] = pfirst_seq if part == 0 else pgen
        in_maps.append(m)
    res = run_bass_kernel_spmd(nc, in_maps, core_ids=list(range(NCORES)))
    out = np.empty((B, S, D), f)
    for core in range(NCORES):
        b = core // cps
        part = core % cps
        out[b, part * tpc:(part + 1) * tpc] = res.results[core]["out"]
    return out
```
